# Optimizing a Trainium2 kernel written in Bass

```python
import math
import jax, jax.numpy as jnp
from jax import lax
import numpy as np

D_MODEL = 2048
BATCH = 4
SEQ = 4096
DEPTH = 1

MIX_WIDTH = D_MODEL
SSM_WIDTH = MIX_WIDTH // 2
POOL_WIDTH = MIX_WIDTH - SSM_WIDTH
SSM_GROUP = 16
SSM_GROUPS = SSM_WIDTH // SSM_GROUP
SSM_STATE = 64
POOL_WINDOWS = (2, 4, 8, 16)
POOL_GROUPS = len(POOL_WINDOWS)
POOL_GROUP_WIDTH = POOL_WIDTH // POOL_GROUPS
N_MEM = 256
MEM_HEADS = 4
MEM_HEAD_DIM = D_MODEL // MEM_HEADS
D_FF = ((8 * D_MODEL // 3 + 255) // 256) * 256
DT_MIN = 1e-3
DT_MAX = 1e-1
EPS = 1e-6

kernel_name = 'hymba_s5_pool_macaron_block'

F32 = jnp.float32


def rmsnorm(x, g):
    xf = x.astype(F32)
    y = xf * lax.rsqrt(jnp.mean(xf * xf, axis=-1, keepdims=True) + EPS) * g.astype(F32)
    return y.astype(x.dtype)


def swiglu(x, w_gate, w_up, w_down):
    return (jax.nn.silu(x @ w_gate) * (x @ w_up)) @ w_down


def _complex_scan_combine(e1, e2):
    a1r, a1i, b1r, b1i = e1
    a2r, a2i, b2r, b2i = e2
    ar = a2r * a1r - a2i * a1i
    ai = a2r * a1i + a2i * a1r
    br = a2r * b1r - a2i * b1i + b2r
    bi = a2r * b1i + a2i * b1r + b2i
    return (ar, ai, br, bi)


def s5_mixer(u, a_re, a_im, log_dt, b_re, b_im, c_re, c_im, d_skip, w_glu, b_glu):
    bsz, L, _ = u.shape
    uf = u.astype(F32)
    ug = uf.reshape(bsz, L, SSM_GROUPS, SSM_GROUP)
    dt = jnp.exp(log_dt.astype(F32))[:, None]
    lr, li = a_re.astype(F32), a_im.astype(F32)
    mag = jnp.exp(lr * dt)
    abar_re = mag * jnp.cos(li * dt)
    abar_im = mag * jnp.sin(li * dt)
    nr, ni = abar_re - 1.0, abar_im
    den = lr * lr + li * li
    fr = (nr * lr + ni * li) / den
    fi = (ni * lr - nr * li) / den
    br, bi = b_re.astype(F32), b_im.astype(F32)
    bbar_re = fr[..., None] * br - fi[..., None] * bi
    bbar_im = fr[..., None] * bi + fi[..., None] * br
    bu_re = jnp.einsum('blgh,gph->blgp', ug, bbar_re)
    bu_im = jnp.einsum('blgh,gph->blgp', ug, bbar_im)
    a_full_re = jnp.broadcast_to(abar_re, bu_re.shape)
    a_full_im = jnp.broadcast_to(abar_im, bu_im.shape)
    _, _, s_re, s_im = lax.associative_scan(
        _complex_scan_combine, (a_full_re, a_full_im, bu_re, bu_im), axis=1)
    y = (jnp.einsum('blgp,ghp->blgh', s_re, c_re.astype(F32))
         - jnp.einsum('blgp,ghp->blgh', s_im, c_im.astype(F32)))
    y = y.reshape(bsz, L, SSM_WIDTH) + d_skip.astype(F32) * uf
    y = jax.nn.gelu(y)
    y = y * jax.nn.sigmoid(y @ w_glu.astype(F32) + b_glu.astype(F32))
    return y.astype(u.dtype)


def pool_mixer(v, w_pool, pool_scale):
    bsz, L, _ = v.shape
    vf = v.astype(F32).reshape(bsz, L, POOL_GROUPS, POOL_GROUP_WIDTH)
    csum = jnp.cumsum(vf, axis=1)
    t = jnp.arange(L)
    pooled = []
    for gi, w in enumerate(POOL_WINDOWS):
        cg = csum[:, :, gi]
        shifted = jnp.pad(cg, ((0, 0), (w, 0), (0, 0)))[:, :L]
        cnt = jnp.minimum(t + 1, w).astype(F32)[None, :, None]
        pooled.append((cg - shifted) / cnt)
    pooled = jnp.stack(pooled, axis=2) - vf
    z = jnp.einsum('blgc,gcd->blgd', pooled, w_pool.astype(F32))
    z = z.reshape(bsz, L, POOL_WIDTH) * pool_scale.astype(F32)
    return z.astype(v.dtype)


def memory_cross_attention(h, memn, w_q, w_k, w_v, w_o):
    bsz, L, _ = h.shape
    q = (h @ w_q).reshape(bsz, L, MEM_HEADS, MEM_HEAD_DIM)
    k = (memn @ w_k).reshape(bsz, N_MEM, MEM_HEADS, MEM_HEAD_DIM)
    v = (memn @ w_v).reshape(bsz, N_MEM, MEM_HEADS, MEM_HEAD_DIM)
    s = jnp.einsum('blhd,bmhd->bhlm', q.astype(F32), k.astype(F32)) * (MEM_HEAD_DIM ** -0.5)
    p = jax.nn.softmax(s, axis=-1).astype(h.dtype)
    o = jnp.einsum('bhlm,bmhd->blhd', p, v).reshape(bsz, L, D_MODEL)
    return o @ w_o


def setup_inputs(seed: int = 0) -> dict:
    key = jax.random.key(seed)
    ks = iter(jax.random.split(key, 40))
    nrm = lambda shape, scale: jax.random.normal(next(ks), shape, F32) * scale
    gain = lambda shape: 1.0 + 0.02 * jax.random.normal(next(ks), shape, F32)
    Ly = DEPTH
    G, P, H = SSM_GROUPS, SSM_STATE, SSM_GROUP
    inp = {}
    inp['x'] = nrm((BATCH, SEQ, D_MODEL), 1.0)
    inp['mem'] = nrm((BATCH, N_MEM, D_MODEL), 1.0)
    inp['g_ffn1'] = gain((Ly, D_MODEL))
    inp['w1_gate'] = nrm((Ly, D_MODEL, D_FF), D_MODEL ** -0.5)
    inp['w1_up'] = nrm((Ly, D_MODEL, D_FF), D_MODEL ** -0.5)
    inp['w1_down'] = nrm((Ly, D_FF, D_MODEL), D_FF ** -0.5)
    inp['g_mix'] = gain((Ly, D_MODEL))
    inp['w_in'] = nrm((Ly, D_MODEL, MIX_WIDTH), D_MODEL ** -0.5)
    inp['ssm_a_re'] = -0.5 + nrm((Ly, G, P), 0.01)
    inp['ssm_a_im'] = math.pi * jnp.arange(P, dtype=F32)[None, None, :] + nrm((Ly, G, P), 0.01)
    inp['ssm_log_dt'] = jax.random.uniform(next(ks), (Ly, G), F32, math.log(DT_MIN), math.log(DT_MAX))
    inp['ssm_b_re'] = nrm((Ly, G, P, H), (2 * H) ** -0.5)
    inp['ssm_b_im'] = nrm((Ly, G, P, H), (2 * H) ** -0.5)
    inp['ssm_c_re'] = nrm((Ly, G, H, P), (2 * P) ** -0.5)
    inp['ssm_c_im'] = nrm((Ly, G, H, P), (2 * P) ** -0.5)
    inp['ssm_d'] = nrm((Ly, SSM_WIDTH), 1.0)
    inp['w_glu'] = nrm((Ly, SSM_WIDTH, SSM_WIDTH), SSM_WIDTH ** -0.5)
    inp['b_glu'] = nrm((Ly, SSM_WIDTH), 0.02)
    inp['w_pool'] = nrm((Ly, POOL_GROUPS, POOL_GROUP_WIDTH, POOL_GROUP_WIDTH), POOL_GROUP_WIDTH ** -0.5)
    inp['pool_scale'] = 1.0 + nrm((Ly, POOL_WIDTH), 0.1)
    inp['g_out_ssm'] = gain((Ly, SSM_WIDTH))
    inp['g_out_pool'] = gain((Ly, POOL_WIDTH))
    inp['w_out'] = nrm((Ly, MIX_WIDTH, D_MODEL), MIX_WIDTH ** -0.5)
    inp['g_xattn'] = gain((Ly, D_MODEL))
    inp['g_mem'] = gain((Ly, D_MODEL))
    inp['w_q'] = nrm((Ly, D_MODEL, D_MODEL), D_MODEL ** -0.5)
    inp['w_k'] = nrm((Ly, D_MODEL, D_MODEL), D_MODEL ** -0.5)
    inp['w_v'] = nrm((Ly, D_MODEL, D_MODEL), D_MODEL ** -0.5)
    inp['w_o'] = nrm((Ly, D_MODEL, D_MODEL), D_MODEL ** -0.5)
    inp['g_ffn2'] = gain((Ly, D_MODEL))
    inp['w2_gate'] = nrm((Ly, D_MODEL, D_FF), D_MODEL ** -0.5)
    inp['w2_up'] = nrm((Ly, D_MODEL, D_FF), D_MODEL ** -0.5)
    inp['w2_down'] = nrm((Ly, D_FF, D_MODEL), D_FF ** -0.5)
    inp['g_final'] = gain((D_MODEL,))
    return inp


def reference(x, mem, g_ffn1, w1_gate, w1_up, w1_down, g_mix, w_in,
              ssm_a_re, ssm_a_im, ssm_log_dt, ssm_b_re, ssm_b_im, ssm_c_re, ssm_c_im,
              ssm_d, w_glu, b_glu, w_pool, pool_scale, g_out_ssm, g_out_pool, w_out,
              g_xattn, g_mem, w_q, w_k, w_v, w_o,
              g_ffn2, w2_gate, w2_up, w2_down, g_final):
    h = x
    for l in range(DEPTH):
        h = h + 0.5 * swiglu(rmsnorm(h, g_ffn1[l]), w1_gate[l], w1_up[l], w1_down[l])
        u = rmsnorm(h, g_mix[l]) @ w_in[l]
        u_ssm, u_pool = u[..., :SSM_WIDTH], u[..., SSM_WIDTH:]
        y_ssm = s5_mixer(u_ssm, ssm_a_re[l], ssm_a_im[l], ssm_log_dt[l], ssm_b_re[l], ssm_b_im[l],
                         ssm_c_re[l], ssm_c_im[l], ssm_d[l], w_glu[l], b_glu[l])
        y_pool = pool_mixer(u_pool, w_pool[l], pool_scale[l])
        merged = jnp.concatenate([rmsnorm(y_ssm, g_out_ssm[l]), rmsnorm(y_pool, g_out_pool[l])], axis=-1)
        h = h + merged @ w_out[l]
        memn = rmsnorm(mem, g_mem[l])
        h = h + memory_cross_attention(rmsnorm(h, g_xattn[l]), memn, w_q[l], w_k[l], w_v[l], w_o[l])
        h = h + 0.5 * swiglu(rmsnorm(h, g_ffn2[l]), w2_gate[l], w2_up[l], w2_down[l])
    return rmsnorm(h, g_final)
```

```python
import numpy as np
from contextlib import ExitStack
import concourse.bass as bass
import concourse.mybir as mybir
from concourse.bass_utils import run_bass_kernel_spmd

F32 = mybir.dt.float32
BF16 = mybir.dt.bfloat16
I32 = mybir.dt.int32
ALU = mybir.AluOpType
AF = mybir.ActivationFunctionType

NT = 2048
ST = 1024
D = 2048
DFF = 5632
NFC = 44
TWO_PI = 6.283185307179586
PI = 3.141592653589793


class Buf:
    __slots__ = ("t", "name", "lw", "rd", "dsem", "dcnt")

    def __init__(self, t, name):
        self.t = t
        self.name = name
        self.lw = None
        self.rd = {}
        self.dsem = None
        self.dcnt = 0

    def __getitem__(self, idx):
        return self.t[idx]


class K:
    def __init__(self, nc, es):
        self.nc = nc
        self.es = es
        self.eng = {"pe": nc.tensor, "act": nc.scalar, "dve": nc.vector,
                    "pool": nc.gpsimd, "sp": nc.sync}
        self.sem = {}
        self.cnt = {}
        for k in self.eng:
            self.sem[k] = es.enter_context(nc.semaphore("s_" + k))
            self.cnt[k] = 0
        self.waited = {}
        self.prog = {k_: [] for k_ in self.eng}
        self.dbufs = []
        self.pe_open = False
        self.psb = []
        self.psi = 0
        self.wsb = []
        self.wsi = 0

    def sb(self, name, shape, dt):
        t = self.es.enter_context(self.nc.sbuf_tensor(name, shape, dt))
        return Buf(t, name)

    def view(self, t, name):
        return Buf(t, name)

    def psum(self):
        b = self.psb[self.psi % len(self.psb)]
        self.psi += 1
        return b

    def wslot(self):
        b = self.wsb[self.wsi % len(self.wsb)]
        self.wsi += 1
        return b

    def _wait(self, e, dep):
        key, c = dep
        if key == "pe" and e == "pe":
            return
        w = self.waited.get((e, key), 0)
        if w >= c:
            return
        sem = self.sem[key] if isinstance(key, str) else key[1]
        self.eng[e].wait_ge(sem, c)
        self.prog[e].append(("w", id(sem), c))
        self.waited[(e, key)] = c

    def _deps(self, e, reads, writes, dma_key=None):
        for b in reads:
            if b.lw is not None:
                self._wait(e, b.lw)
        for b in writes:
            if b.lw is not None and b.lw[0] != dma_key:
                self._wait(e, b.lw)
            for r in list(b.rd.items()):
                self._wait(e, r)

    def op(self, e, fn, reads=(), writes=(), inc=True):
        self._deps(e, reads, writes)
        ins = fn(self.eng[e])
        if inc:
            self.cnt[e] += 1
            ins.then_inc(self.sem[e], 1)
            self.prog[e].append(("i", id(self.sem[e]), 1))
            c = self.cnt[e]
            if e == "pe":
                self.pe_open = False
        else:
            assert e == "pe"
            c = self.cnt[e] + 1
            self.pe_open = True
        for b in writes:
            b.lw = (e, c)
            b.rd = {}
        for b in reads:
            b.rd[e] = c
        return ins

    def dma(self, e, out_ap, in_ap, dst, src, **kw):
        if dst.dsem is None:
            dst.dsem = self.es.enter_context(self.nc.semaphore("d_" + dst.name))
            self.dbufs.append(dst)
        key = ("d", dst.dsem)
        self._deps(e, [src] if src is not None else [], [dst], dma_key=key)
        ins = self.eng[e].dma_start(out=out_ap, in_=in_ap, **kw)
        ins.then_inc(dst.dsem, 16)
        self.prog[e].append(("i", id(dst.dsem), 16))
        dst.dcnt += 16
        if dst.lw is None or dst.lw[0] != key:
            dst.rd = {}
        dst.lw = (key, dst.dcnt)
        if src is not None:
            src.rd[key] = dst.dcnt
        return ins

    def barrier(self):
        assert not self.pe_open
        for e in self.eng:
            for k2 in self.eng:
                if k2 != e and self.cnt[k2] > 0:
                    self._wait(e, (k2, self.cnt[k2]))
            for b in self.dbufs:
                if b.dcnt > 0:
                    self._wait(e, (("d", b.dsem), b.dcnt))

    def simulate(self):
        pc = {e: 0 for e in self.prog}
        val = {}
        progress = True
        while progress:
            progress = False
            for e, pr in self.prog.items():
                while pc[e] < len(pr):
                    kind, sid, v = pr[pc[e]]
                    if kind == "w":
                        if val.get(sid, 0) >= v:
                            pc[e] += 1
                            progress = True
                        else:
                            break
                    else:
                        val[sid] = val.get(sid, 0) + v
                        pc[e] += 1
                        progress = True
        stuck = {e: (pc[e], len(pr)) for e, pr in self.prog.items() if pc[e] < len(pr)}
        return stuck, {e: len(pr) for e, pr in self.prog.items()}, dict(self.cnt)

    def finish(self, bufs):
        for b in bufs:
            if b.lw is not None:
                self._wait("sp", b.lw)


def build_nc(dbg=()):
    nc = bass.Bass("TRN2", target_bir_lowering=False)

    def din(name, shape, dt=F32):
        return nc.dram_tensor(name, shape, dt, kind="ExternalInput").ap()

    xT = din("xT", [D, NT])
    memT = din("memT", [D, 256])
    w1g = din("w1_gate", [D, DFF]); w1u = din("w1_up", [D, DFF]); w1d = din("w1_down", [DFF, D])
    w2g = din("w2_gate", [D, DFF]); w2u = din("w2_up", [D, DFF]); w2d = din("w2_down", [DFF, D])
    w_in = din("w_in", [D, D]); w_out = din("w_out", [D, D])
    w_q = din("w_q", [D, D]); w_k = din("w_k", [D, D]); w_v = din("w_v", [D, D]); w_o = din("w_o", [D, D])
    w_glu = din("w_glu", [1024, 1024]); w_pool = din("w_pool", [4, 256, 256])
    gv_d = din("gv", [128, 136])
    cl_d = din("ssm_cl", [5, 128, 512])
    sl3_d = din("ssm_sl3", [3, 64, 64])
    sl4_d = din("ssm_sl4", [4, 64, 1024])
    mask_d = din("mask8", [128, 8])
    flag_d = din("flag", [128, 1])
    icnt_d = din("invcnt", [128, 64])
    yT = nc.dram_tensor("yT", [D, NT], F32, kind="ExternalOutput").ap()

    H1d = nc.dram_tensor("H1d", [D, NT], F32).ap()
    UPd = nc.dram_tensor("UPd", [1024, NT], BF16).ap()
    YLd = nc.dram_tensor("YLd", [1024, NT], F32).ap()
    MRGd = nc.dram_tensor("MRGd", [D, NT], BF16).ap()
    WPd = nc.dram_tensor("WPd", [8, 128, 16384], BF16).ap()
    WCd = nc.dram_tensor("WCd", [8, 64, 32768], BF16).ap()
    KBd = nc.dram_tensor("KBd", [8, 128, 2048], BF16).ap()
    PSTd = nc.dram_tensor("PSTd", [64, 64 * 2 * 128], F32).ap()
    XINd = nc.dram_tensor("XINd", [128, 256], F32)
    XOUTd = nc.dram_tensor("XOUTd", [256, 256], F32)

    dbg_out = {}
    for name, shape in dbg:
        dbg_out[name] = nc.dram_tensor("dbg_" + name, shape, F32, kind="ExternalOutput").ap()

    with ExitStack() as es:
        k = K(nc, es)
        arena = es.enter_context(nc.sbuf_tensor("arena", [128, 65536], BF16))
        k.psb = [Buf(es.enter_context(nc.psum_tensor("ps%d" % i, [128, 512], F32)), "ps%d" % i) for i in range(8)]
        k.wsb = [k.sb("ws%d" % i, [128, 4096], BF16) for i in range(4)]
        GV = k.sb("GV", [128, 136], F32)
        ones = k.sb("ones", [128, 128], BF16)
        mask8 = k.sb("mask8s", [128, 8], F32)
        flag = k.sb("flags", [128, 1], F32)
        icnt = k.sb("icnt", [128, 64], F32)
        sqt = [k.sb("sq%d" % i, [128, 512], BF16) for i in range(4)]
        rst = [k.sb("rs%d" % i, [128, 512], F32) for i in range(2)]
        stg = [k.sb("stg%d" % i, [128, 512], F32) for i in range(4)]
        stb = [k.sb("stb%d" % i, [128, 512], BF16) for i in range(4)]
        cnts = {"sq": 0, "rs": 0, "stg": 0, "stb": 0}

        def rot(lst, key):
            b = lst[cnts[key] % len(lst)]
            cnts[key] += 1
            return b

        dH1 = k.view(H1d, "H1d"); dUP = k.view(UPd, "UPd"); dYL = k.view(YLd, "YLd"); dMRG = k.view(MRGd, "MRGd")
        dWP = k.view(WPd, "WPd"); dWC = k.view(WCd, "WCd"); dKB = k.view(KBd, "KBd"); dPST = k.view(PSTd, "PSTd")
        dXIN = k.view(XINd, "XINd"); dXOUT = k.view(XOUTd, "XOUTd"); dY = k.view(yT, "yT")
        dIN = k.view(xT, "inputs")
        dDBG = k.view(None, "dbg")

        GC = {"ffn1": 0, "mix": 16, "xattn": 32, "mem": 48, "ffn2": 64, "final": 80,
              "ossm": 96, "opool": 104, "dskip": 112, "bglu": 120, "pscale": 128}

        k.dma("sp", GV[:], gv_d, GV, dIN)
        k.dma("sp", mask8[:], mask_d, mask8, dIN)
        k.dma("sp", flag[:], flag_d, flag, dIN)
        k.dma("sp", icnt[:], icnt_d, icnt, dIN)
        k.op("dve", lambda e: e.memset(ones[:], 1.0), [], [ones])

        def av(off, nbytes, dt, pat=None, p0=0, p1=128, **kw):
            esz = 4 if dt in (F32, I32) else 2
            v = arena[p0:p1, off // 2:(off + nbytes) // 2]
            if dt != BF16:
                v = v.bitcast(dt)
            if pat:
                v = v.rearrange(pat, **kw)
            return v

        Hv = av(0, 65536, F32, "p (c t) -> p c t", c=16)
        XNv = av(65536, 32768, BF16, "p (c t) -> p c t", c=16)
        ABv = av(98304, 32768, BF16, "p (c t) -> p c t", c=16)
        H = k.view(Hv, "H"); XN = k.view(XNv, "XN"); AB = k.view(ABv, "AB")

        def dump(name, ap, buf):
            if name in dbg_out:
                k.dma("sp", dbg_out[name], ap, dDBG, buf)

        def rmsnorm(src_buf, src, dst_buf, dst, nk, gcol, ntt, dn, tw=512):
            for tt in range(ntt):
                ps = k.psum()
                for kc in range(nk):
                    sq = rot(sqt, "sq")
                    k.op("act", lambda e: e.activation(out=sq[:, 0:tw], in_=src(kc, tt), func=AF.Square), [src_buf], [sq])
                    k.op("pe", lambda e: e.matmul(ps[:, 0:tw], lhsT=ones[:], rhs=sq[:, 0:tw], start=(kc == 0), stop=(kc == nk - 1)),
                         [ones, sq], [ps], inc=True)
                rs = rot(rst, "rs")
                k.op("act", lambda e: e.activation(out=rs[:, 0:tw], in_=ps[:, 0:tw], func=AF.Sqrt, bias=EPSB[:, 0:1], scale=1.0 / dn), [ps, EPS], [rs])
                k.op("dve", lambda e: e.reciprocal(out=rs[:, 0:tw], in_=rs[:, 0:tw]), [rs], [rs])
                for kc in range(nk):
                    k.op("dve", lambda e: e.scalar_tensor_tensor(out=dst(kc, tt), in0=src(kc, tt), scalar=GV[:, gcol + kc:gcol + kc + 1],
                                                                  in1=rs[:, 0:tw], op0=ALU.mult, op1=ALU.mult), [src_buf, rs, GV], [dst_buf])

        EPS = k.sb("eps", [128, 1], F32)
        EPSB = EPS
        k.op("dve", lambda e: e.memset(EPS[:], 1e-6), [], [EPS])

        def wload(w_ap, rows, r0, nkc, c0, ncols, name="w"):
            ws = k.wslot()
            v = ws[:, 0:nkc * ncols].rearrange("p (c n) -> p c n", c=nkc)
            src = w_ap[r0:r0 + nkc * 128, c0:c0 + ncols].rearrange("(c p) n -> p c n", p=128)
            k.dma("pool", v, src, ws, dIN)
            return ws, v

        def ffn(wg, wu, wd):
            parts = [(0, 16), (16, 32), (32, 44)]
            for (f0, f1) in parts:
                nf = f1 - f0
                for fp in range(f0, f1, 2):
                    wgs, wgv = wload(wg, D, 0, 16, fp * 128, 256)
                    wus, wuv = wload(wu, D, 0, 16, fp * 128, 256)
                    for fl in range(2):
                        f = fp + fl
                        for tt in range(2):
                            pg = k.psum(); pu = k.psum()
                            for kc in range(16):
                                k.op("pe", lambda e: e.matmul(pg[:], lhsT=wgv[:, kc, fl * 128:(fl + 1) * 128], rhs=XNv[:, kc, tt * 512:(tt + 1) * 512],
                                                               start=(kc == 0), stop=(kc == 15)), [wgs, XN], [pg], inc=(kc == 15))
                            for kc in range(16):
                                k.op("pe", lambda e: e.matmul(pu[:], lhsT=wuv[:, kc, fl * 128:(fl + 1) * 128], rhs=XNv[:, kc, tt * 512:(tt + 1) * 512],
                                                               start=(kc == 0), stop=(kc == 15)), [wus, XN], [pu], inc=(kc == 15))
                            sg = rot(stb, "stb")
                            k.op("act", lambda e: e.activation(out=sg[:], in_=pg[:], func=AF.Silu), [pg], [sg])
                            k.op("dve", lambda e: e.tensor_tensor(out=ABv[:, f - f0, tt * 512:(tt + 1) * 512], in0=sg[:], in1=pu[:], op=ALU.mult),
                                 [sg, pu], [AB])
                for dp in range(8):
                    wds, wdv = wload(wd, DFF, f0 * 128, nf, dp * 256, 256)
                    for dl in range(2):
                        d = dp * 2 + dl
                        for tt in range(2):
                            ps = k.psum()
                            for fc in range(nf):
                                k.op("pe", lambda e: e.matmul(ps[:], lhsT=wdv[:, fc, dl * 128:(dl + 1) * 128], rhs=ABv[:, fc, tt * 512:(tt + 1) * 512],
                                                               start=(fc == 0), stop=(fc == nf - 1)), [wds, AB], [ps], inc=(fc == nf - 1))
                            hs = Hv[:, d, tt * 512:(tt + 1) * 512]
                            k.op("dve", lambda e: e.scalar_tensor_tensor(out=hs, in0=ps[:], scalar=0.5, in1=hs, op0=ALU.mult, op1=ALU.add),
                                 [ps, H], [H])

        def proj(w_ap, rhs_buf, rhsv, evac, ntt=2, tw=512):
            for op_ in range(8):
                ws, wv = wload(w_ap, D, 0, 16, op_ * 256, 256)
                for ol in range(2):
                    oc = op_ * 2 + ol
                    for tt in range(ntt):
                        ps = k.psum()
                        for kc in range(16):
                            k.op("pe", lambda e: e.matmul(ps[:, 0:tw], lhsT=wv[:, kc, ol * 128:(ol + 1) * 128], rhs=rhsv[:, kc, tt * tw:(tt + 1) * tw],
                                                           start=(kc == 0), stop=(kc == 15)), [ws, rhs_buf], [ps], inc=(kc == 15))
                        evac(oc, tt, ps)

        o = [0]

        def al(nbytes, dt, pat=None, p1=128, **kw):
            v = av(o[0], nbytes, dt, pat, 0, p1, **kw)
            o[0] += nbytes
            return v

        P0 = k.view(None, "P0")
        CLin = al(5 * 2048, F32, "p (a n) -> p a n", a=5)
        k.dma("sp", CLin, cl_d.rearrange("a p n -> p a n"), P0, dIN)
        names = ["dt", "mag", "ang", "kf", "sn", "cs", "ar", "ai", "nr", "den", "fr", "fi", "t1", "t2", "pr", "pi", "bbr", "bbi", "t3", "t4"]
        cl = {n: al(2048, F32) for n in names}
        cli = al(2048, I32)
        WPc = al(32768, BF16, "p (q t r s) -> p q t r s", q=8, t=16, r=2)

        def abar(L, li_, lr_, ldt_, ki, eng="dve"):
            A = lambda fn: k.op("act", fn, [P0], [P0])
            V = lambda fn: k.op("dve", fn, [P0], [P0])
            A(lambda e: e.activation(out=L["dt"], in_=ldt_, func=AF.Exp))
            V(lambda e: e.tensor_tensor(out=L["mag"], in0=lr_, in1=L["dt"], op=ALU.mult))
            A(lambda e: e.activation(out=L["mag"], in_=L["mag"], func=AF.Exp))
            V(lambda e: e.tensor_tensor(out=L["ang"], in0=li_, in1=L["dt"], op=ALU.mult))
            V(lambda e: e.tensor_scalar(out=L["kf"], in0=L["ang"], scalar1=1.0 / TWO_PI, scalar2=None, op0=ALU.mult))
            V(lambda e: e.tensor_copy(out=ki, in_=L["kf"]))
            V(lambda e: e.tensor_copy(out=L["kf"], in_=ki))
            V(lambda e: e.scalar_tensor_tensor(out=L["ang"], in0=L["kf"], scalar=-TWO_PI, in1=L["ang"], op0=ALU.mult, op1=ALU.add))
            for nm, sh in (("sn", 0.0), ("cs", PI / 2)):
                V(lambda e: e.tensor_scalar(out=L["t1"], in0=L["ang"], scalar1=sh, scalar2=None, op0=ALU.add))
                V(lambda e: e.tensor_scalar(out=L["t2"], in0=L["t1"], scalar1=PI, scalar2=-TWO_PI, op0=ALU.is_gt, op1=ALU.mult))
                V(lambda e: e.tensor_tensor(out=L["t1"], in0=L["t1"], in1=L["t2"], op=ALU.add))
                V(lambda e: e.tensor_scalar(out=L["t2"], in0=L["t1"], scalar1=-PI, scalar2=TWO_PI, op0=ALU.is_lt, op1=ALU.mult))
                V(lambda e: e.tensor_tensor(out=L["t1"], in0=L["t1"], in1=L["t2"], op=ALU.add))
                V(lambda e: e.tensor_scalar(out=L["t1"], in0=L["t1"], scalar1=PI, scalar2=-PI, op0=ALU.min, op1=ALU.max))
                A(lambda e: e.activation(out=L[nm], in_=L["t1"], func=AF.Sin))
            V(lambda e: e.tensor_tensor(out=L["ar"], in0=L["mag"], in1=L["cs"], op=ALU.mult))
            V(lambda e: e.tensor_tensor(out=L["ai"], in0=L["mag"], in1=L["sn"], op=ALU.mult))
            V(lambda e: e.tensor_scalar(out=L["nr"], in0=L["ar"], scalar1=-1.0, scalar2=None, op0=ALU.add))
            V(lambda e: e.tensor_tensor(out=L["den"], in0=lr_, in1=lr_, op=ALU.mult))
            V(lambda e: e.tensor_tensor(out=L["t1"], in0=li_, in1=li_, op=ALU.mult))
            V(lambda e: e.tensor_tensor(out=L["den"], in0=L["den"], in1=L["t1"], op=ALU.add))
            V(lambda e: e.reciprocal(out=L["den"], in_=L["den"]))
            V(lambda e: e.tensor_tensor(out=L["t1"], in0=L["nr"], in1=lr_, op=ALU.mult))
            V(lambda e: e.tensor_tensor(out=L["t2"], in0=L["ai"], in1=li_, op=ALU.mult))
            V(lambda e: e.tensor_tensor(out=L["t1"], in0=L["t1"], in1=L["t2"], op=ALU.add))
            V(lambda e: e.tensor_tensor(out=L["fr"], in0=L["t1"], in1=L["den"], op=ALU.mult))
            V(lambda e: e.tensor_tensor(out=L["t1"], in0=L["ai"], in1=lr_, op=ALU.mult))
            V(lambda e: e.tensor_tensor(out=L["t2"], in0=L["nr"], in1=li_, op=ALU.mult))
            V(lambda e: e.tensor_tensor(out=L["t1"], in0=L["t1"], in1=L["t2"], op=ALU.subtract))
            V(lambda e: e.tensor_tensor(out=L["fi"], in0=L["t1"], in1=L["den"], op=ALU.mult))

        def cmul(L, outr, outi, ar_, ai_, br_, bi_, t1, t2):
            V = lambda fn: k.op("dve", fn, [P0], [P0])
            V(lambda e: e.tensor_tensor(out=t1, in0=ar_, in1=br_, op=ALU.mult))
            V(lambda e: e.tensor_tensor(out=t2, in0=ai_, in1=bi_, op=ALU.mult))
            V(lambda e: e.tensor_tensor(out=t1, in0=t1, in1=t2, op=ALU.subtract))
            V(lambda e: e.tensor_tensor(out=t2, in0=ar_, in1=bi_, op=ALU.mult))
            V(lambda e: e.tensor_tensor(out=outi, in0=ai_, in1=br_, op=ALU.mult))
            V(lambda e: e.tensor_tensor(out=outi, in0=outi, in1=t2, op=ALU.add))
            V(lambda e: e.tensor_copy(out=outr, in_=t1))

        V0 = lambda fn: k.op("dve", fn, [P0], [P0])
        abar(cl, CLin[:, 1, :], CLin[:, 0, :], CLin[:, 2, :], cli)
        cmul(cl, cl["bbr"], cl["bbi"], cl["fr"], cl["fi"], CLin[:, 3, :], CLin[:, 4, :], cl["t1"], cl["t2"])
        V0(lambda e: e.memset(cl["pr"], 1.0))
        V0(lambda e: e.memset(cl["pi"], 0.0))
        for kk in range(16):
            tau = 15 - kk
            cmul(cl, cl["t3"], cl["t4"], cl["pr"], cl["pi"], cl["bbr"], cl["bbi"], cl["t1"], cl["t2"])
            V0(lambda e: e.tensor_copy(out=WPc[:, :, tau, 0, :], in_=cl["t3"].rearrange("p (q s) -> p q s", q=8)))
            V0(lambda e: e.tensor_copy(out=WPc[:, :, tau, 1, :], in_=cl["t4"].rearrange("p (q s) -> p q s", q=8)))
            if kk < 15:
                cmul(cl, cl["pr"], cl["pi"], cl["pr"], cl["pi"], cl["ar"], cl["ai"], cl["t1"], cl["t2"])
        ZT = al(16384, BF16)
        V0(lambda e: e.memset(ZT, 0.0))
        for q in range(8):
            for hh in range(2):
                k.dma("sp", WPd[q, :, hh * 8192:(hh + 1) * 8192], ZT, dWP, P0)
        k.barrier()
        WPdv = WPd.rearrange("q c (g x) -> q c g x", g=8)
        for gi in range(8):
            for q in range(8):
                k.dma("sp", WPdv[q, 16 * gi:16 * gi + 16, gi, :], WPc[16 * gi:16 * gi + 16, q].rearrange("p t r s -> p (t r s)"), dWP, P0)

        k.barrier()
        o[0] = 0
        SL3 = al(3 * 256, F32, "p (a n) -> p a n", a=3)
        k.dma("sp", SL3[0:64], sl3_d.rearrange("a p n -> p a n"), P0, dIN)
        SL4 = al(4 * 4096, F32, "p (a n) -> p a n", a=4)
        k.dma("sp", SL4[0:64], sl4_d.rearrange("a p n -> p a n"), P0, dIN)
        sl = {n: al(256, F32)[0:64] for n in names}
        sli = al(256, I32)[0:64]
        abar(sl, SL3[0:64, 1, :], SL3[0:64, 0, :], SL3[0:64, 2, :], sli)
        PWr = al(17 * 256, F32, "p (k g) -> p k g", k=17)[0:64]
        PWi = al(17 * 256, F32, "p (k g) -> p k g", k=17)[0:64]
        V0(lambda e: e.memset(PWr[:, 0, :], 1.0))
        V0(lambda e: e.memset(PWi[:, 0, :], 0.0))
        for kk in range(16):
            cmul(sl, PWr[:, kk + 1, :], PWi[:, kk + 1, :], PWr[:, kk, :], PWi[:, kk, :], sl["ar"], sl["ai"], sl["t1"], sl["t2"])
        AT = k.sb("AT", [64, 2, 64], F32)
        k.op("dve", lambda e: e.tensor_copy(out=AT[:, 0, :], in_=PWr[:, 16, :]), [P0], [AT])
        k.op("dve", lambda e: e.tensor_copy(out=AT[:, 1, :], in_=PWi[:, 16, :]), [P0], [AT])
        Cre = SL4[0:64, 0, :].rearrange("p (g h) -> p g h", h=16)
        Cim = SL4[0:64, 1, :].rearrange("p (g h) -> p g h", h=16)
        Bres = SL4[0:64, 2, :].rearrange("p (g h) -> p g h", h=16)
        Bims = SL4[0:64, 3, :].rearrange("p (g h) -> p g h", h=16)
        frb = sl["fr"].unsqueeze(2).broadcast_to([64, 64, 16])
        fib = sl["fi"].unsqueeze(2).broadcast_to([64, 64, 16])
        BBr = al(4096, F32, "p (g h) -> p g h", h=16)[0:64]
        BBi = al(4096, F32, "p (g h) -> p g h", h=16)[0:64]
        TA = al(4096, F32, "p (g h) -> p g h", h=16)[0:64]
        TB = al(4096, F32, "p (g h) -> p g h", h=16)[0:64]
        V0(lambda e: e.tensor_tensor(out=TA, in0=Bres, in1=frb, op=ALU.mult))
        V0(lambda e: e.tensor_tensor(out=TB, in0=Bims, in1=fib, op=ALU.mult))
        V0(lambda e: e.tensor_tensor(out=BBr, in0=TA, in1=TB, op=ALU.subtract))
        V0(lambda e: e.tensor_tensor(out=TA, in0=Bims, in1=frb, op=ALU.mult))
        V0(lambda e: e.tensor_tensor(out=TB, in0=Bres, in1=fib, op=ALU.mult))
        V0(lambda e: e.tensor_tensor(out=TA, in0=TA, in1=TB, op=ALU.add))
        V0(lambda e: e.tensor_scalar(out=BBi, in0=TA, scalar1=-1.0, scalar2=None, op0=ALU.mult))
        Dr = al(8 * 17 * 16 * 4, F32, "p (g k h) -> p g k h", g=8, k=17)[0:64]
        Di = al(8 * 17 * 16 * 4, F32, "p (g k h) -> p g k h", g=8, k=17)[0:64]
        T1 = al(8 * 17 * 16 * 4, F32, "p (g k h) -> p g k h", g=8, k=17)[0:64]
        T2 = al(8 * 17 * 16 * 4, F32, "p (g k h) -> p g k h", g=8, k=17)[0:64]
        BPr = al(8 * 128 * 4, F32, "p (g n) -> p g n", g=8)[0:64]
        BPi = al(8 * 128 * 4, F32, "p (g n) -> p g n", g=8)[0:64]
        V0(lambda e: e.memset(BPr, 0.0))
        V0(lambda e: e.memset(BPi, 0.0))
        WCt = al(32768, BF16, "p (g t r n) -> p g t r n", g=4, t=16, r=2)[0:64]
        Kc = al(1024, F32, "p (k h) -> p k h", k=16)
        KB = al(4096, BF16, "p (k g h) -> p k g h", k=16, g=8)
        assert o[0] <= 131072, o[0]
        for q in range(8):
            gs = slice(8 * q, 8 * q + 8)
            pwr = PWr[:, :, gs].rearrange("p k g -> p g k").unsqueeze(3).broadcast_to([64, 8, 17, 16])
            pwi = PWi[:, :, gs].rearrange("p k g -> p g k").unsqueeze(3).broadcast_to([64, 8, 17, 16])
            cre = Cre[:, gs, :].unsqueeze(2).broadcast_to([64, 8, 17, 16])
            cim = Cim[:, gs, :].unsqueeze(2).broadcast_to([64, 8, 17, 16])
            V0(lambda e: e.tensor_tensor(out=T1, in0=cre, in1=pwr, op=ALU.mult))
            V0(lambda e: e.tensor_tensor(out=T2, in0=cim, in1=pwi, op=ALU.mult))
            V0(lambda e: e.tensor_tensor(out=Dr, in0=T1, in1=T2, op=ALU.subtract))
            V0(lambda e: e.tensor_tensor(out=T1, in0=cre, in1=pwi, op=ALU.mult))
            V0(lambda e: e.tensor_tensor(out=T2, in0=cim, in1=pwr, op=ALU.mult))
            V0(lambda e: e.tensor_tensor(out=Di, in0=T1, in1=T2, op=ALU.add))
            for gi in range(8):
                V0(lambda e: e.tensor_copy(out=BPr[:, gi, 16 * gi:16 * gi + 16], in_=BBr[:, 8 * q + gi, :]))
                V0(lambda e: e.tensor_copy(out=BPi[:, gi, 16 * gi:16 * gi + 16], in_=BBi[:, 8 * q + gi, :]))
            ps = k.psum()
            n = 0
            for gi in range(8):
                for (bp, dd) in ((BPr, Dr), (BPi, Di)):
                    k.op("pe", lambda e: e.matmul(ps[:, 0:256], lhsT=bp[:, gi, :], rhs=dd[:, gi, 0:16, :], start=(n == 0), stop=(n == 15)),
                         [P0], [ps], inc=(n == 15))
                    n += 1
            k.op("act", lambda e: e.activation(out=Kc, in_=ps[:, 0:256].rearrange("p (k h) -> p k h", k=16), func=AF.Copy), [ps], [P0])
            for kk in range(16):
                V0(lambda e: e.tensor_tensor(out=KB[:, kk], in0=Kc[:, kk].unsqueeze(1).broadcast_to([128, 8, 16]),
                                             in1=mask8[:, :].unsqueeze(2).broadcast_to([128, 8, 16]), op=ALU.mult))
            k.dma("sp", KBd[q], KB.rearrange("p k g h -> p (k g h)"), dKB, P0)
            for gh in range(2):
                V0(lambda e: e.memset(WCt, 0.0))
                for gl in range(4):
                    gi = 4 * gh + gl
                    V0(lambda e: e.tensor_copy(out=WCt[:, gl, :, 0, 16 * gi:16 * gi + 16], in_=Dr[:, gi, 1:17, :]))
                    V0(lambda e: e.tensor_scalar(out=WCt[:, gl, :, 1, 16 * gi:16 * gi + 16], in0=Di[:, gi, 1:17, :], scalar1=-1.0, scalar2=None, op0=ALU.mult))
                k.dma("sp", WCd[q, :, gh * 16384:(gh + 1) * 16384], WCt.rearrange("p g t r n -> p (g t r n)"), dWC, P0)
        k.barrier()

        xv = xT.rearrange("(c p) t -> p c t", p=128)
        H1v = H1d.rearrange("(c p) t -> p c t", p=128)
        UPv = UPd.rearrange("(c p) t -> p c t", p=128)
        YLv = YLd.rearrange("(c p) t -> p c t", p=128)
        MRGv = MRGd.rearrange("(c p) t -> p c t", p=128)
        PSTv = PSTd.rearrange("p (g r c) -> p g r c", g=64, r=2)
        for s in range(2):
            t0 = s * ST
            for c4 in range(4):
                k.dma("sp", Hv[:, 4 * c4:4 * c4 + 4, :], xv[:, 4 * c4:4 * c4 + 4, t0:t0 + ST], H, dIN)
            rmsnorm(H, lambda kc, tt: Hv[:, kc, tt * 512:(tt + 1) * 512], XN, lambda kc, tt: XNv[:, kc, tt * 512:(tt + 1) * 512], 16, GC["ffn1"], 2, 2048.0)
            ffn(w1g, w1u, w1d)
            k.dma("sp", H1v[:, :, t0:t0 + ST], Hv, dH1, H)
            rmsnorm(H, lambda kc, tt: Hv[:, kc, tt * 512:(tt + 1) * 512], XN, lambda kc, tt: XNv[:, kc, tt * 512:(tt + 1) * 512], 16, GC["mix"], 2, 2048.0)

            def ev_in(oc, tt, ps):
                if oc < 8:
                    k.op("act", lambda e: e.activation(out=ABv[:, oc, tt * 512:(tt + 1) * 512], in_=ps[:], func=AF.Copy), [ps], [AB])
                else:
                    sb_ = rot(stb, "stb")
                    k.op("act", lambda e: e.activation(out=sb_[:], in_=ps[:], func=AF.Copy), [ps], [sb_])
                    k.dma("sp", UPv[:, oc - 8, t0 + tt * 512:t0 + (tt + 1) * 512], sb_[:], dUP, sb_)
            proj(w_in, XN, XNv, ev_in)
            for q in range(8):
                kbs = k.wslot()
                kbv = kbs[:, 0:2048].rearrange("p (k n) -> p k n", k=16)
                k.dma("sp", kbs[:, 0:2048], KBd[q], kbs, dKB)
                for tt in range(2):
                    ps = k.psum()
                    uu = ABv[:, q, tt * 512:(tt + 1) * 512].rearrange("p (c j) -> p c j", j=16)
                    pp = ps[:, :].rearrange("p (c j) -> p c j", j=16)
                    for kk in range(16):
                        k.op("pe", lambda e: e.matmul(pp[:, :, kk:16], lhsT=kbv[:, kk, :], rhs=uu[:, :, 0:16 - kk], start=(kk == 0), stop=(kk == 15)),
                             [kbs, AB], [ps], inc=(kk == 15))
                    yl = rot(stg, "stg")
                    k.op("dve", lambda e: e.scalar_tensor_tensor(out=yl[:], in0=ABv[:, q, tt * 512:(tt + 1) * 512], scalar=GV[:, GC["dskip"] + q:GC["dskip"] + q + 1],
                                                                  in1=ps[:], op0=ALU.mult, op1=ALU.add), [AB, ps, GV], [yl])
                    k.dma("sp", YLv[:, q, t0 + tt * 512:t0 + (tt + 1) * 512], yl[:], dYL, yl)
                uq = ABv[:, q, :].rearrange("p (c j) -> p c j", j=16)
                for gp in range(4):
                    wps = k.wslot()
                    wpv = wps[:, :].rearrange("p (g t r s) -> p g t r s", g=2, t=16, r=2)
                    k.dma("sp", wps[:, :], WPd[q, :, gp * 4096:(gp + 1) * 4096], wps, dWP)
                    ps = k.psum()
                    for gl in range(2):
                        for ri in range(2):
                            r_ = gl * 2 + ri
                            for tau in range(16):
                                k.op("pe", lambda e: e.matmul(ps[0:64, r_ * 64:(r_ + 1) * 64], lhsT=wpv[:, gl, tau, ri, :], rhs=uq[:, :, tau],
                                                               start=(tau == 0), stop=(tau == 15)), [wps, AB], [ps], inc=(tau == 15))
                    pst = rot(stg, "stg")
                    k.op("act", lambda e: e.activation(out=pst[0:64, 0:256], in_=ps[0:64, 0:256], func=AF.Copy), [ps], [pst])
                    g0 = 8 * q + 2 * gp
                    k.dma("sp", PSTv[:, g0:g0 + 2, :, s * 64:(s + 1) * 64], pst[0:64, 0:256].rearrange("p (g r c) -> p g r c", g=2, r=2), dPST, pst)
        k.barrier()

        o[0] = 0
        SSb = al(64 * 2 * 129 * 2 + 0, BF16, "p (g r c) -> p g r c", g=64, r=2)[0:64]
        SS = al(64 * 2 * 129 * 4, F32, "p (g r c) -> p g r c", g=64, r=2)[0:64]
        PX = k.view(None, "PX")
        XS = k.sb("XS", [128, 256], F32)
        XR = k.sb("XR", [128, 256], F32)
        sct = [k.sb("sc%d" % i, [64, 64, 2], F32) for i in range(2)]
        A1 = k.sb("A1", [64, 64, 2], F32)
        A2 = k.sb("A2", [64, 64, 2], F32)
        k.op("dve", lambda e: e.tensor_copy(out=A1[:, :, 0], in_=AT[:, 0, :]), [AT], [A1])
        k.op("dve", lambda e: e.tensor_copy(out=A1[:, :, 1], in_=AT[:, 0, :]), [AT], [A1])
        k.op("dve", lambda e: e.tensor_scalar(out=A2[:, :, 0], in0=AT[:, 1, :], scalar1=-1.0, scalar2=None, op0=ALU.mult), [AT], [A2])
        k.op("dve", lambda e: e.tensor_copy(out=A2[:, :, 1], in_=AT[:, 1, :]), [AT], [A2])

        def scan():
            VX = lambda fn: k.op("dve", fn, [PX, A1, A2], [PX])
            for c in range(128):
                prev = SS[:, :, :, c]
                cur = SS[:, :, :, c + 1]
                t1, t2 = sct[0], sct[1]
                VX(lambda e: e.tensor_tensor(out=t1[:], in0=prev, in1=A1[:], op=ALU.mult))
                VX(lambda e: e.tensor_tensor(out=t2[:, :, 0], in0=SS[:, :, 1, c], in1=A2[:, :, 0], op=ALU.mult))
                VX(lambda e: e.tensor_tensor(out=t2[:, :, 1], in0=SS[:, :, 0, c], in1=A2[:, :, 1], op=ALU.mult))
                VX(lambda e: e.tensor_tensor(out=t1[:], in0=t1[:], in1=t2[:], op=ALU.add))
                VX(lambda e: e.tensor_tensor(out=cur, in0=cur, in1=t1[:], op=ALU.add))

        k.dma("sp", SS[:, :, :, 1:129], PSTv, PX, dPST)
        k.op("dve", lambda e: e.memset(SS[:, :, :, 0], 0.0), [PX], [PX])
        scan()
        k.op("dve", lambda e: e.memset(XS[:], 0.0), [], [XS])
        k.op("dve", lambda e: e.tensor_copy(out=XS[0:64, 0:128].rearrange("p (g r) -> p g r", r=2), in_=SS[:, :, :, 128]), [PX], [XS])
        hb = rot(stb, "stb")
        k.dma("sp", hb[:, 0:128].rearrange("p (q j) -> p q j", q=8), UPv[:, :, NT - 16:NT], hb, dUP)
        k.op("dve", lambda e: e.tensor_copy(out=XS[:, 128:256], in_=hb[:, 0:128]), [hb], [XS])
        k.dma("sp", XINd.ap(), XS[:], dXIN, XS)
        ccsem = es.enter_context(nc.semaphore("ccsem"))
        k._deps("pool", [dXIN], [dXOUT])
        nc.gpsimd.collective_compute("AllGather", ALU.bypass, replica_groups=[[0, 1], [2, 3], [4, 5], [6, 7]],
                                     ins=[XINd.ap().opt()], outs=[XOUTd.ap().opt()]).then_inc(ccsem)
        k.prog["pool"].append(("i", id(ccsem), 1))
        dXOUT.lw = (("d", ccsem), 1)
        dXOUT.rd = {}
        dXIN.rd[("d", ccsem)] = 1
        k.dma("sp", XR[:], XOUTd.ap()[0:128, :], XR, dXOUT)
        k.op("dve", lambda e: e.tensor_scalar(out=XR[:], in0=XR[:], scalar1=flag[:, 0:1], scalar2=None, op0=ALU.mult), [XR, flag], [XR])
        k.dma("sp", SS[:, :, :, 1:129], PSTv, PX, dPST)
        k.op("dve", lambda e: e.tensor_copy(out=SS[:, :, :, 0], in_=XR[0:64, 0:128].rearrange("p (g r) -> p g r", r=2)), [XR, PX], [PX])
        scan()
        k.op("dve", lambda e: e.tensor_copy(out=SSb[:, 0:32], in_=SS[:, 0:32]), [PX], [PX])
        k.op("act", lambda e: e.activation(out=SSb[:, 32:64], in_=SS[:, 32:64], func=AF.Copy), [PX], [PX])

        k.barrier()
        o[0] = 33024
        YGb = al(32768, BF16, "p (q t) -> p q t", q=8)
        WGL = al(16384, BF16, "p (c n) -> p c n", c=8)
        YLt = al(8192, F32)
        Yt = al(8192, F32)
        Tt = al(8192, F32)
        assert o[0] <= 131072, o[0]
        BYG = k.view(None, "YGb"); BWGL = k.view(None, "WGL"); BYL = k.view(None, "YLt"); BY = k.view(None, "Yt"); BT = k.view(None, "Tt")
        k.dma("pool", WGL, w_glu.rearrange("(c p) n -> p c n", p=128), BWGL, dIN)
        for q in range(8):
            k.dma("sp", YLt, YLv[:, q, :], BYL, dYL)
            pss = []
            for tq in range(4):
                ps = k.psum()
                wl = []
                for gh in range(2):
                    wcs = k.wslot()
                    wcv = wcs[0:64, :].rearrange("p (g t r n) -> p g t r n", g=4, t=4, r=2)
                    src = WCd[q].rearrange("p (g t r n) -> p g t r n", g=8, t=16, r=2)[:, 4 * gh:4 * gh + 4, 4 * tq:4 * tq + 4]
                    k.dma("sp", wcv, src, wcs, dWC)
                    wl.append((wcs, wcv))
                for tl in range(4):
                    n = 0
                    for gh in range(2):
                        wcs, wcv = wl[gh]
                        for gl in range(4):
                            gi = 4 * gh + gl
                            for ri in range(2):
                                k.op("pe", lambda e: e.matmul(ps[:, tl * 128:(tl + 1) * 128], lhsT=wcv[:, gl, tl, ri, :], rhs=SSb[:, 8 * q + gi, ri, 0:128],
                                                               start=(n == 0), stop=(n == 15)), [wcs, PX], [ps], inc=(n == 15))
                                n += 1
                pss.append(ps)
            for tq in range(4):
                ps = pss[tq]
                yv = Yt.rearrange("p (c j) -> p c j", j=16)[:, :, 4 * tq:4 * tq + 4]
                lv = YLt.rearrange("p (c j) -> p c j", j=16)[:, :, 4 * tq:4 * tq + 4]
                pv = ps[:, :].rearrange("p (j c) -> p c j", j=4)
                k.op("dve", lambda e: e.tensor_tensor(out=yv, in0=lv, in1=pv, op=ALU.add), [BYL, ps], [BY])
            k.op("act", lambda e: e.activation(out=Tt, in_=Yt, func=AF.Square), [BY], [BT])
            k.op("dve", lambda e: e.tensor_scalar(out=Tt, in0=Tt, scalar1=0.044715, scalar2=1.0, op0=ALU.mult, op1=ALU.add), [BT], [BT])
            k.op("dve", lambda e: e.tensor_tensor(out=Tt, in0=Tt, in1=Yt, op=ALU.mult), [BT, BY], [BT])
            k.op("act", lambda e: e.activation(out=Tt, in_=Tt, func=AF.Tanh, scale=0.7978845608028654), [BT], [BT])
            k.op("dve", lambda e: e.tensor_scalar(out=Tt, in0=Tt, scalar1=1.0, scalar2=0.5, op0=ALU.add, op1=ALU.mult), [BT], [BT])
            k.op("dve", lambda e: e.tensor_tensor(out=YGb[:, q, :], in0=Tt, in1=Yt, op=ALU.mult), [BT, BY], [BYG])
        o1 = o[0]
        YM = al(16384, F32, "p (c t) -> p c t", c=8)
        BYM = k.view(None, "YM")
        for tt in range(4):
            for oc in range(8):
                ps = k.psum()
                for kc in range(8):
                    k.op("pe", lambda e: e.matmul(ps[:], lhsT=WGL[:, kc, oc * 128:(oc + 1) * 128], rhs=YGb[:, kc, tt * 512:(tt + 1) * 512],
                                                   start=(kc == 0), stop=(kc == 7)), [BWGL, BYG], [ps], inc=(kc == 7))
                sg = rot(stg, "stg")
                k.op("act", lambda e: e.activation(out=sg[:], in_=ps[:], func=AF.Sigmoid, bias=GV[:, GC["bglu"] + oc:GC["bglu"] + oc + 1]), [ps, GV], [sg])
                k.op("dve", lambda e: e.tensor_tensor(out=YM[:, oc, :], in0=YGb[:, oc, tt * 512:(tt + 1) * 512], in1=sg[:], op=ALU.mult), [BYG, sg], [BYM])
            mt = {}

            def dstf(kc, t_):
                b = rot(stb, "stb")
                mt[kc] = b
                return b[:]
            ps = k.psum()
            for kc in range(8):
                sq = rot(sqt, "sq")
                k.op("act", lambda e: e.activation(out=sq[:], in_=YM[:, kc, :], func=AF.Square), [BYM], [sq])
                k.op("pe", lambda e: e.matmul(ps[:], lhsT=ones[:], rhs=sq[:], start=(kc == 0), stop=(kc == 7)), [ones, sq], [ps], inc=True)
            rs = rot(rst, "rs")
            k.op("act", lambda e: e.activation(out=rs[:], in_=ps[:], func=AF.Sqrt, bias=EPS[:, 0:1], scale=1.0 / 1024.0), [ps, EPS], [rs])
            k.op("dve", lambda e: e.reciprocal(out=rs[:], in_=rs[:]), [rs], [rs])
            for kc in range(8):
                b = rot(stb, "stb")
                k.op("dve", lambda e: e.scalar_tensor_tensor(out=b[:], in0=YM[:, kc, :], scalar=GV[:, GC["ossm"] + kc:GC["ossm"] + kc + 1], in1=rs[:],
                                                              op0=ALU.mult, op1=ALU.mult), [BYM, rs, GV], [b])
                k.dma("sp", MRGv[:, kc, tt * 512:(tt + 1) * 512], b[:], dMRG, b)
        k.barrier()
        o[0] = 0
        PL = al(32768, BF16, "p (c t) -> p c t", c=8)
        Vt = al(2064 * 4, F32)
        Sa = al(2064 * 4, F32)
        Sb_ = al(2064 * 4, F32)
        Ub = al(4096, BF16)
        ZT2 = al(16384, F32, "p (c t) -> p c t", c=8)
        assert o[0] <= 131072
        BPL = k.view(None, "PL"); BV = k.view(None, "Vt"); BSa = k.view(None, "Sa"); BSb = k.view(None, "Sb"); BU = k.view(None, "Ub"); BZ = k.view(None, "ZT2")
        WIN = (2, 4, 8, 16)
        XRh = XR[:, 128:256].rearrange("p (q j) -> p q j", q=8)
        for pc in range(8):
            gi = pc // 2
            w = WIN[gi]
            k.dma("sp", Ub, UPv[:, pc, :], BU, dUP)
            k.op("act", lambda e: e.activation(out=Vt[:, 16:2064], in_=Ub, func=AF.Copy), [BU], [BV])
            k.op("dve", lambda e: e.tensor_copy(out=Vt[:, 0:16], in_=XRh[:, pc, :]), [XR], [BV])
            cur, curb = Vt, BV
            sh = 1
            bufs2 = [(Sa, BSa), (Sb_, BSb)]
            bi = 0
            while sh < w:
                nxt, nxtb = bufs2[bi]
                bi ^= 1
                k.op("dve", lambda e: e.tensor_tensor(out=nxt[:, sh:2064], in0=cur[:, sh:2064], in1=cur[:, 0:2064 - sh], op=ALU.add), [curb], [nxtb])
                cur, curb = nxt, nxtb
                sh *= 2
            k.op("dve", lambda e: e.scalar_tensor_tensor(out=PL[:, pc, :], in0=cur[:, 16:2064], scalar=1.0 / w, in1=Vt[:, 16:2064], op0=ALU.mult, op1=ALU.subtract),
                 [curb, BV], [BPL])
            nxt, nxtb = bufs2[bi]
            k.op("dve", lambda e: e.tensor_tensor(out=nxt[:, 0:16], in0=cur[:, 16:32], in1=icnt[:, 16 * gi:16 * gi + 16], op=ALU.mult), [curb, icnt], [nxtb])
            k.op("dve", lambda e: e.tensor_tensor(out=PL[:, pc, 0:16], in0=nxt[:, 0:16], in1=Vt[:, 16:32], op=ALU.subtract), [nxtb, BV], [BPL])
        wpl = []
        for gi in range(4):
            ws = k.wslot()
            v = ws[:, 0:512].rearrange("p (c n) -> p c n", c=2)
            k.dma("pool", v, w_pool[gi].rearrange("(c p) n -> p c n", p=128), ws, dIN)
            wpl.append((ws, v))
        for tt in range(4):
            for gi in range(4):
                ws, v = wpl[gi]
                for dc in range(2):
                    ps = k.psum()
                    for cc in range(2):
                        k.op("pe", lambda e: e.matmul(ps[:], lhsT=v[:, cc, dc * 128:(dc + 1) * 128], rhs=PL[:, 2 * gi + cc, tt * 512:(tt + 1) * 512],
                                                       start=(cc == 0), stop=(cc == 1)), [ws, BPL], [ps], inc=(cc == 1))
                    oc = 2 * gi + dc
                    k.op("act", lambda e: e.activation(out=ZT2[:, oc, :], in_=ps[:], func=AF.Copy, scale=GV[:, GC["pscale"] + oc:GC["pscale"] + oc + 1]), [ps, GV], [BZ])
            ps = k.psum()
            for kc in range(8):
                sq = rot(sqt, "sq")
                k.op("act", lambda e: e.activation(out=sq[:], in_=ZT2[:, kc, :], func=AF.Square), [BZ], [sq])
                k.op("pe", lambda e: e.matmul(ps[:], lhsT=ones[:], rhs=sq[:], start=(kc == 0), stop=(kc == 7)), [ones, sq], [ps], inc=True)
            rs = rot(rst, "rs")
            k.op("act", lambda e: e.activation(out=rs[:], in_=ps[:], func=AF.Sqrt, bias=EPS[:, 0:1], scale=1.0 / 1024.0), [ps, EPS], [rs])
            k.op("dve", lambda e: e.reciprocal(out=rs[:], in_=rs[:]), [rs], [rs])
            for kc in range(8):
                b = rot(stb, "stb")
                k.op("dve", lambda e: e.scalar_tensor_tensor(out=b[:], in0=ZT2[:, kc, :], scalar=GV[:, GC["opool"] + kc:GC["opool"] + kc + 1], in1=rs[:],
                                                              op0=ALU.mult, op1=ALU.mult), [BZ, rs, GV], [b])
                k.dma("sp", MRGv[:, 8 + kc, tt * 512:(tt + 1) * 512], b[:], dMRG, b)
        k.barrier()

        MN = k.view(av(98304, 8192, BF16, "p (c m) -> p c m", c=16), "MN")
        KT = k.sb("KT", [128, 16, 256], BF16)
        VM = k.sb("VM", [128, 2, 2048], BF16)
        MF = Hv[:, :, 0:256]
        k.dma("sp", MF, memT.rearrange("(c p) m -> p c m", p=128), H, dIN)
        rmsnorm(H, lambda kc, tt: Hv[:, kc, 0:256], MN, lambda kc, tt: MN[:, kc, :], 16, GC["mem"], 1, 2048.0, tw=256)

        def ev_k(oc, tt, ps):
            k.op("act", lambda e: e.activation(out=KT[:, oc, :], in_=ps[:, 0:256], func=AF.Copy), [ps], [KT])
        proj(w_k, MN, MN, ev_k, ntt=1, tw=256)
        for op_ in range(8):
            ws, wv = wload(w_v, D, 0, 16, op_ * 256, 256)
            for mc in range(2):
                ps = k.psum()
                for kc in range(16):
                    k.op("pe", lambda e: e.matmul(ps[:, 0:256], lhsT=MN[:, kc, mc * 128:(mc + 1) * 128], rhs=wv[:, kc, :], start=(kc == 0), stop=(kc == 15)),
                         [ws, MN], [ps], inc=(kc == 15))
                k.op("act", lambda e: e.activation(out=VM[:, mc, op_ * 256:(op_ + 1) * 256], in_=ps[:, 0:256], func=AF.Copy), [ps], [VM])

        yv_ = yT.rearrange("(c p) t -> p c t", p=128)
        Et = [k.sb("Et%d" % i, [128, 2, 512], BF16) for i in range(2)]
        ecnt = [0]
        for s in range(2):
            t0 = s * ST
            k.dma("sp", Hv, H1v[:, :, t0:t0 + ST], H, dH1)
            k.dma("sp", XNv, MRGv[:, :, t0:t0 + ST], XN, dMRG)

            def ev_add(oc, tt, ps):
                hs = Hv[:, oc, tt * 512:(tt + 1) * 512]
                k.op("dve", lambda e: e.tensor_tensor(out=hs, in0=hs, in1=ps[:], op=ALU.add), [H, ps], [H])
            proj(w_out, XN, XNv, ev_add)
            rmsnorm(H, lambda kc, tt: Hv[:, kc, tt * 512:(tt + 1) * 512], XN, lambda kc, tt: XNv[:, kc, tt * 512:(tt + 1) * 512], 16, GC["xattn"], 2, 2048.0)

            def ev_q(oc, tt, ps):
                k.op("act", lambda e: e.activation(out=ABv[:, oc, tt * 512:(tt + 1) * 512], in_=ps[:], func=AF.Copy, scale=512.0 ** -0.5), [ps], [AB])
            proj(w_q, XN, XNv, ev_q)
            for hh in range(4):
                for tt in range(2):
                    et = Et[ecnt[0] % 2]; ecnt[0] += 1
                    for mc in range(2):
                        ps = k.psum()
                        for dc in range(4):
                            k.op("pe", lambda e: e.matmul(ps[:], lhsT=KT[:, hh * 4 + dc, mc * 128:(mc + 1) * 128], rhs=ABv[:, hh * 4 + dc, tt * 512:(tt + 1) * 512],
                                                           start=(dc == 0), stop=(dc == 3)), [KT, AB], [ps], inc=(dc == 3))
                        k.op("act", lambda e: e.activation(out=et[:, mc, :], in_=ps[:], func=AF.Exp), [ps], [et])
                    ps2 = k.psum()
                    for mc in range(2):
                        k.op("pe", lambda e: e.matmul(ps2[:], lhsT=ones[:], rhs=et[:, mc, :], start=(mc == 0), stop=(mc == 1)), [ones, et], [ps2], inc=(mc == 1))
                    rs = rot(rst, "rs")
                    k.op("dve", lambda e: e.reciprocal(out=rs[:], in_=ps2[:]), [ps2], [rs])
                    for dvc in range(4):
                        ps3 = k.psum()
                        for mc in range(2):
                            k.op("pe", lambda e: e.matmul(ps3[:], lhsT=VM[:, mc, hh * 512 + dvc * 128:hh * 512 + (dvc + 1) * 128], rhs=et[:, mc, :],
                                                           start=(mc == 0), stop=(mc == 1)), [VM, et], [ps3], inc=(mc == 1))
                        k.op("dve", lambda e: e.tensor_tensor(out=XNv[:, hh * 4 + dvc, tt * 512:(tt + 1) * 512], in0=ps3[:], in1=rs[:], op=ALU.mult), [ps3, rs], [XN])
            proj(w_o, XN, XNv, ev_add)
            rmsnorm(H, lambda kc, tt: Hv[:, kc, tt * 512:(tt + 1) * 512], XN, lambda kc, tt: XNv[:, kc, tt * 512:(tt + 1) * 512], 16, GC["ffn2"], 2, 2048.0)
            ffn(w2g, w2u, w2d)
            for tt in range(2):
                ps = k.psum()
                for kc in range(16):
                    sq = rot(sqt, "sq")
                    k.op("act", lambda e: e.activation(out=sq[:], in_=Hv[:, kc, tt * 512:(tt + 1) * 512], func=AF.Square), [H], [sq])
                    k.op("pe", lambda e: e.matmul(ps[:], lhsT=ones[:], rhs=sq[:], start=(kc == 0), stop=(kc == 15)), [ones, sq], [ps], inc=True)
                rs = rot(rst, "rs")
                k.op("act", lambda e: e.activation(out=rs[:], in_=ps[:], func=AF.Sqrt, bias=EPS[:, 0:1], scale=1.0 / 2048.0), [ps, EPS], [rs])
                k.op("dve", lambda e: e.reciprocal(out=rs[:], in_=rs[:]), [rs], [rs])
                for kc in range(16):
                    b = rot(stg, "stg")
                    k.op("dve", lambda e: e.scalar_tensor_tensor(out=b[:], in0=Hv[:, kc, tt * 512:(tt + 1) * 512], scalar=GV[:, GC["final"] + kc:GC["final"] + kc + 1],
                                                                  in1=rs[:], op0=ALU.mult, op1=ALU.mult), [H, rs, GV], [b])
                    k.dma("sp", yv_[:, kc, t0 + tt * 512:t0 + (tt + 1) * 512], b[:], dY, b)
        k.finish([dY, dDBG])
        build_nc.sim = k.simulate()
    return nc


def _vec(v):
    v = np.asarray(v, np.float32).reshape(-1, 128)
    return np.ascontiguousarray(v.T)


def prep_inputs(inp):
    L = 0
    f = lambda a: np.ascontiguousarray(np.asarray(a, np.float32))
    x = f(inp["x"]); mem = f(inp["mem"])
    gv = np.concatenate([
        _vec(inp["g_ffn1"][L]), _vec(inp["g_mix"][L]), _vec(inp["g_xattn"][L]), _vec(inp["g_mem"][L]),
        _vec(inp["g_ffn2"][L]), _vec(inp["g_final"]), _vec(inp["g_out_ssm"][L]), _vec(inp["g_out_pool"][L]),
        _vec(inp["ssm_d"][L]), _vec(inp["b_glu"][L]), _vec(inp["pool_scale"][L])], axis=1)
    gv = np.ascontiguousarray(gv, np.float32)
    assert gv.shape == (128, 136)
    a_re = f(inp["ssm_a_re"][L]); a_im = f(inp["ssm_a_im"][L]); ldt = f(inp["ssm_log_dt"][L])
    b_re = f(inp["ssm_b_re"][L]); b_im = f(inp["ssm_b_im"][L])
    c_re = f(inp["ssm_c_re"][L]); c_im = f(inp["ssm_c_im"][L])

    def cl_gp(a):
        a4 = a.reshape(8, 8, 64)
        r = np.broadcast_to(a4[:, :, None, :], (8, 8, 16, 64))
        return np.ascontiguousarray(r.transpose(1, 2, 0, 3).reshape(128, 512))

    def cl_b(b):
        b5 = b.reshape(8, 8, 64, 16)
        return np.ascontiguousarray(b5.transpose(1, 3, 0, 2).reshape(128, 512))
    ldt_gp = np.broadcast_to(ldt[:, None], (64, 64))
    ssm_cl = np.stack([cl_gp(a_re), cl_gp(a_im), cl_gp(ldt_gp), cl_b(b_re), cl_b(b_im)]).astype(np.float32)
    ssm_sl3 = np.stack([a_re.T, a_im.T, ldt_gp.T]).astype(np.float32)
    ssm_sl3 = np.ascontiguousarray(ssm_sl3)
    sl_c = lambda c_: np.ascontiguousarray(c_.transpose(2, 0, 1).reshape(64, 1024))
    sl_b = lambda b_: np.ascontiguousarray(b_.transpose(1, 0, 2).reshape(64, 1024))
    ssm_sl4 = np.stack([sl_c(c_re), sl_c(c_im), sl_b(b_re), sl_b(b_im)]).astype(np.float32)
    mask8 = np.zeros((128, 8), np.float32)
    for c in range(128):
        mask8[c, c // 16] = 1.0
    shared = {
        "w1_gate": f(inp["w1_gate"][L]), "w1_up": f(inp["w1_up"][L]), "w1_down": f(inp["w1_down"][L]),
        "w2_gate": f(inp["w2_gate"][L]), "w2_up": f(inp["w2_up"][L]), "w2_down": f(inp["w2_down"][L]),
        "w_in": f(inp["w_in"][L]), "w_out": f(inp["w_out"][L]), "w_q": f(inp["w_q"][L]), "w_k": f(inp["w_k"][L]),
        "w_v": f(inp["w_v"][L]), "w_o": f(inp["w_o"][L]), "w_glu": f(inp["w_glu"][L]), "w_pool": f(inp["w_pool"][L]),
        "gv": gv, "ssm_cl": ssm_cl, "ssm_sl3": ssm_sl3, "ssm_sl4": ssm_sl4, "mask8": mask8,
    }
    in_maps = []
    for c in range(8):
        b, half = c // 2, c % 2
        m = dict(shared)
        m["xT"] = np.ascontiguousarray(x[b, half * NT:(half + 1) * NT, :].T)
        m["memT"] = np.ascontiguousarray(mem[b].T)
        m["flag"] = np.full((128, 1), float(half), np.float32)
        ic = np.zeros((128, 4, 16), np.float32)
        for gi, w in enumerate((2, 4, 8, 16)):
            for t in range(16):
                ic[:, gi, t] = 1.0 / (min(t + 1, w) if half == 0 else w)
        m["invcnt"] = ic.reshape(128, 64)
        in_maps.append(m)
    return in_maps


_NC = {}


def kernel(**inputs):
    in_maps = prep_inputs(inputs)
    if "nc" not in _NC:
        _NC["nc"] = build_nc()
    res = run_bass_kernel_spmd(_NC["nc"], in_maps, core_ids=list(range(8)))
    out = np.empty((4, 4096, D), np.float32)
    for c in range(8):
        b, half = c // 2, c % 2
        out[b, half * NT:(half + 1) * NT, :] = res.results[c]["yT"].T
    return out
```

```python
import numpy as np
from contextlib import ExitStack
import concourse.bass as bass
import concourse.mybir as mybir
from concourse.bass_utils import run_bass_kernel_spmd

F32 = mybir.dt.float32
BF16 = mybir.dt.bfloat16
I32 = mybir.dt.int32
ALU = mybir.AluOpType
AF = mybir.ActivationFunctionType

NT = 2048
ST = 1024
D = 2048
DFF = 5632
NFC = 44
TWO_PI = 6.283185307179586
PI = 3.141592653589793


class Buf:
    __slots__ = ("t", "name", "lw", "rd", "dsem", "dcnt")

    def __init__(self, t, name):
        self.t = t
        self.name = name
        self.lw = None
        self.rd = {}
        self.dsem = None
        self.dcnt = 0

    def __getitem__(self, idx):
        return self.t[idx]


class K:
    def __init__(self, nc, es):
        self.nc = nc
        self.es = es
        self.eng = {"pe": nc.tensor, "act": nc.scalar, "dve": nc.vector,
                    "pool": nc.gpsimd, "sp": nc.sync}
        self.sem = {}
        self.cnt = {}
        for k in self.eng:
            self.sem[k] = es.enter_context(nc.semaphore("s_" + k))
            self.cnt[k] = 0
        self.waited = {}
        self.prog = {k_: [] for k_ in self.eng}
        self.dbufs = []
        self.pe_open = False
        self.psb = []
        self.psi = 0
        self.wsb = []
        self.wsi = 0

    def sb(self, name, shape, dt):
        t = self.es.enter_context(self.nc.sbuf_tensor(name, shape, dt))
        return Buf(t, name)

    def view(self, t, name):
        return Buf(t, name)

    def psum(self):
        b = self.psb[self.psi % len(self.psb)]
        self.psi += 1
        return b

    def wslot(self):
        b = self.wsb[self.wsi % len(self.wsb)]
        self.wsi += 1
        return b

    def _wait(self, e, dep):
        key, c = dep
        if key == "pe" and e == "pe":
            return
        w = self.waited.get((e, key), 0)
        if w >= c:
            return
        sem = self.sem[key] if isinstance(key, str) else key[1]
        self.eng[e].wait_ge(sem, c)
        self.prog[e].append(("w", id(sem), c))
        self.waited[(e, key)] = c

    def _deps(self, e, reads, writes, dma_key=None):
        for b in reads:
            if b.lw is not None:
                self._wait(e, b.lw)
        for b in writes:
            if b.lw is not None and b.lw[0] != dma_key:
                self._wait(e, b.lw)
            for r in list(b.rd.items()):
                self._wait(e, r)

    def op(self, e, fn, reads=(), writes=(), inc=True):
        self._deps(e, reads, writes)
        ins = fn(self.eng[e])
        if inc:
            self.cnt[e] += 1
            ins.then_inc(self.sem[e], 1)
            self.prog[e].append(("i", id(self.sem[e]), 1))
            c = self.cnt[e]
            if e == "pe":
                self.pe_open = False
        else:
            assert e == "pe"
            c = self.cnt[e] + 1
            self.pe_open = True
        for b in writes:
            b.lw = (e, c)
            b.rd = {}
        for b in reads:
            b.rd[e] = c
        return ins

    def dma(self, e, out_ap, in_ap, dst, src, **kw):
        if dst.dsem is None:
            dst.dsem = self.es.enter_context(self.nc.semaphore("d_" + dst.name))
            self.dbufs.append(dst)
        key = ("d", dst.dsem)
        self._deps(e, [src] if src is not None else [], [dst], dma_key=key)
        ins = self.eng[e].dma_start(out=out_ap, in_=in_ap, **kw)
        ins.then_inc(dst.dsem, 16)
        self.prog[e].append(("i", id(dst.dsem), 16))
        dst.dcnt += 16
        if dst.lw is None or dst.lw[0] != key:
            dst.rd = {}
        dst.lw = (key, dst.dcnt)
        if src is not None:
            src.rd[key] = dst.dcnt
        return ins

    def barrier(self):
        assert not self.pe_open
        for e in self.eng:
            for k2 in self.eng:
                if k2 != e and self.cnt[k2] > 0:
                    self._wait(e, (k2, self.cnt[k2]))
            for b in self.dbufs:
                if b.dcnt > 0:
                    self._wait(e, (("d", b.dsem), b.dcnt))

    def simulate(self):
        pc = {e: 0 for e in self.prog}
        val = {}
        progress = True
        while progress:
            progress = False
            for e, pr in self.prog.items():
                while pc[e] < len(pr):
                    kind, sid, v = pr[pc[e]]
                    if kind == "w":
                        if val.get(sid, 0) >= v:
                            pc[e] += 1
                            progress = True
                        else:
                            break
                    else:
                        val[sid] = val.get(sid, 0) + v
                        pc[e] += 1
                        progress = True
        stuck = {e: (pc[e], len(pr)) for e, pr in self.prog.items() if pc[e] < len(pr)}
        return stuck, {e: len(pr) for e, pr in self.prog.items()}, dict(self.cnt)

    def finish(self, bufs):
        for b in bufs:
            if b.lw is not None:
                self._wait("sp", b.lw)


def build_nc(dbg=()):
    nc = bass.Bass("TRN2", target_bir_lowering=False)

    def din(name, shape, dt=F32):
        return nc.dram_tensor(name, shape, dt, kind="ExternalInput").ap()

    xT = din("xT", [D, NT])
    memT = din("memT", [D, 256])
    w1g = din("w1_gate", [D, DFF]); w1u = din("w1_up", [D, DFF]); w1d = din("w1_down", [DFF, D])
    w2g = din("w2_gate", [D, DFF]); w2u = din("w2_up", [D, DFF]); w2d = din("w2_down", [DFF, D])
    w_in = din("w_in", [D, D]); w_out = din("w_out", [D, D])
    w_q = din("w_q", [D, D]); w_k = din("w_k", [D, D]); w_v = din("w_v", [D, D]); w_o = din("w_o", [D, D])
    w_glu = din("w_glu", [1024, 1024]); w_pool = din("w_pool", [4, 256, 256])
    gv_d = din("gv", [128, 136])
    cl_d = din("ssm_cl", [5, 128, 512])
    sl3_d = din("ssm_sl3", [3, 64, 64])
    sl4_d = din("ssm_sl4", [4, 64, 1024])
    mask_d = din("mask8", [128, 8])
    flag_d = din("flag", [128, 1])
    icnt_d = din("invcnt", [128, 64])
    yT = nc.dram_tensor("yT", [D, NT], F32, kind="ExternalOutput").ap()

    H1d = nc.dram_tensor("H1d", [D, NT], F32).ap()
    UPd = nc.dram_tensor("UPd", [1024, NT], BF16).ap()
    YLd = nc.dram_tensor("YLd", [1024, NT], F32).ap()
    MRGd = nc.dram_tensor("MRGd", [D, NT], BF16).ap()
    WPd = nc.dram_tensor("WPd", [8, 128, 16384], BF16).ap()
    WCd = nc.dram_tensor("WCd", [8, 64, 32768], BF16).ap()
    KBd = nc.dram_tensor("KBd", [8, 128, 2048], BF16).ap()
    PSTd = nc.dram_tensor("PSTd", [64, 64 * 2 * 128], F32).ap()
    XINd = nc.dram_tensor("XINd", [128, 256], F32)
    XOUTd = nc.dram_tensor("XOUTd", [256, 256], F32)

    dbg_out = {}
    for name, shape in dbg:
        dbg_out[name] = nc.dram_tensor("dbg_" + name, shape, F32, kind="ExternalOutput").ap()

    with ExitStack() as es:
        k = K(nc, es)
        arena = es.enter_context(nc.sbuf_tensor("arena", [128, 65536], BF16))
        k.psb = [Buf(es.enter_context(nc.psum_tensor("ps%d" % i, [128, 512], F32)), "ps%d" % i) for i in range(8)]
        k.wsb = [k.sb("ws%d" % i, [128, 4096], BF16) for i in range(4)]
        GV = k.sb("GV", [128, 136], F32)
        ones = k.sb("ones", [128, 128], BF16)
        mask8 = k.sb("mask8s", [128, 8], F32)
        flag = k.sb("flags", [128, 1], F32)
        icnt = k.sb("icnt", [128, 64], F32)
        sqt = [k.sb("sq%d" % i, [128, 512], BF16) for i in range(4)]
        rst = [k.sb("rs%d" % i, [128, 512], F32) for i in range(2)]
        stg = [k.sb("stg%d" % i, [128, 512], F32) for i in range(4)]
        stb = [k.sb("stb%d" % i, [128, 512], BF16) for i in range(4)]
        cnts = {"sq": 0, "rs": 0, "stg": 0, "stb": 0}

        def rot(lst, key):
            b = lst[cnts[key] % len(lst)]
            cnts[key] += 1
            return b

        dH1 = k.view(H1d, "H1d"); dUP = k.view(UPd, "UPd"); dYL = k.view(YLd, "YLd"); dMRG = k.view(MRGd, "MRGd")
        dWP = k.view(WPd, "WPd"); dWC = k.view(WCd, "WCd"); dKB = k.view(KBd, "KBd"); dPST = k.view(PSTd, "PSTd")
        dXIN = k.view(XINd, "XINd"); dXOUT = k.view(XOUTd, "XOUTd"); dY = k.view(yT, "yT")
        dIN = k.view(xT, "inputs")
        dDBG = k.view(None, "dbg")

        GC = {"ffn1": 0, "mix": 16, "xattn": 32, "mem": 48, "ffn2": 64, "final": 80,
              "ossm": 96, "opool": 104, "dskip": 112, "bglu": 120, "pscale": 128}

        k.dma("sp", GV[:], gv_d, GV, dIN)
        k.dma("sp", mask8[:], mask_d, mask8, dIN)
        k.dma("sp", flag[:], flag_d, flag, dIN)
        k.dma("sp", icnt[:], icnt_d, icnt, dIN)
        k.op("dve", lambda e: e.memset(ones[:], 1.0), [], [ones])

        def av(off, nbytes, dt, pat=None, p0=0, p1=128, **kw):
            esz = 4 if dt in (F32, I32) else 2
            v = arena[p0:p1, off // 2:(off + nbytes) // 2]
            if dt != BF16:
                v = v.bitcast(dt)
            if pat:
                v = v.rearrange(pat, **kw)
            return v

        Hv = av(0, 65536, F32, "p (c t) -> p c t", c=16)
        XNv = av(65536, 32768, BF16, "p (c t) -> p c t", c=16)
        ABv = av(98304, 32768, BF16, "p (c t) -> p c t", c=16)
        H = k.view(Hv, "H"); XN = k.view(XNv, "XN"); AB = k.view(ABv, "AB")

        def dump(name, ap, buf):
            if name in dbg_out:
                k.dma("sp", dbg_out[name], ap, dDBG, buf)

        def rmsnorm(src_buf, src, dst_buf, dst, nk, gcol, ntt, dn, tw=512):
            for tt in range(ntt):
                ps = k.psum()
                for kc in range(nk):
                    sq = rot(sqt, "sq")
                    k.op("act", lambda e: e.activation(out=sq[:, 0:tw], in_=src(kc, tt), func=AF.Square), [src_buf], [sq])
                    k.op("pe", lambda e: e.matmul(ps[:, 0:tw], lhsT=ones[:], rhs=sq[:, 0:tw], start=(kc == 0), stop=(kc == nk - 1)),
                         [ones, sq], [ps], inc=True)
                rs = rot(rst, "rs")
                k.op("act", lambda e: e.activation(out=rs[:, 0:tw], in_=ps[:, 0:tw], func=AF.Sqrt, bias=EPSB[:, 0:1], scale=1.0 / dn), [ps, EPS], [rs])
                k.op("dve", lambda e: e.reciprocal(out=rs[:, 0:tw], in_=rs[:, 0:tw]), [rs], [rs])
                for kc in range(nk):
                    k.op("dve", lambda e: e.scalar_tensor_tensor(out=dst(kc, tt), in0=src(kc, tt), scalar=GV[:, gcol + kc:gcol + kc + 1],
                                                                  in1=rs[:, 0:tw], op0=ALU.mult, op1=ALU.mult), [src_buf, rs, GV], [dst_buf])

        EPS = k.sb("eps", [128, 1], F32)
        EPSB = EPS
        k.op("dve", lambda e: e.memset(EPS[:], 1e-6), [], [EPS])

        def wload(w_ap, rows, r0, nkc, c0, ncols, name="w"):
            ws = k.wslot()
            v = ws[:, 0:nkc * ncols].rearrange("p (c n) -> p c n", c=nkc)
            src = w_ap[r0:r0 + nkc * 128, c0:c0 + ncols].rearrange("(c p) n -> p c n", p=128)
            k.dma("pool", v, src, ws, dIN)
            return ws, v

        def ffn(wg, wu, wd):
            parts = [(0, 16), (16, 32), (32, 44)]
            for (f0, f1) in parts:
                nf = f1 - f0
                for fp in range(f0, f1, 2):
                    wgs, wgv = wload(wg, D, 0, 16, fp * 128, 256)
                    wus, wuv = wload(wu, D, 0, 16, fp * 128, 256)
                    for fl in range(2):
                        f = fp + fl
                        for tt in range(2):
                            pg = k.psum(); pu = k.psum()
                            for kc in range(16):
                                k.op("pe", lambda e: e.matmul(pg[:], lhsT=wgv[:, kc, fl * 128:(fl + 1) * 128], rhs=XNv[:, kc, tt * 512:(tt + 1) * 512],
                                                               start=(kc == 0), stop=(kc == 15)), [wgs, XN], [pg], inc=(kc == 15))
                            for kc in range(16):
                                k.op("pe", lambda e: e.matmul(pu[:], lhsT=wuv[:, kc, fl * 128:(fl + 1) * 128], rhs=XNv[:, kc, tt * 512:(tt + 1) * 512],
                                                               start=(kc == 0), stop=(kc == 15)), [wus, XN], [pu], inc=(kc == 15))
                            sg = rot(stb, "stb")
                            k.op("act", lambda e: e.activation(out=sg[:], in_=pg[:], func=AF.Silu), [pg], [sg])
                            k.op("dve", lambda e: e.tensor_tensor(out=ABv[:, f - f0, tt * 512:(tt + 1) * 512], in0=sg[:], in1=pu[:], op=ALU.mult),
                                 [sg, pu], [AB])
                for dp in range(8):
                    wds, wdv = wload(wd, DFF, f0 * 128, nf, dp * 256, 256)
                    for dl in range(2):
                        d = dp * 2 + dl
                        for tt in range(2):
                            ps = k.psum()
                            for fc in range(nf):
                                k.op("pe", lambda e: e.matmul(ps[:], lhsT=wdv[:, fc, dl * 128:(dl + 1) * 128], rhs=ABv[:, fc, tt * 512:(tt + 1) * 512],
                                                               start=(fc == 0), stop=(fc == nf - 1)), [wds, AB], [ps], inc=(fc == nf - 1))
                            hs = Hv[:, d, tt * 512:(tt + 1) * 512]
                            k.op("dve", lambda e: e.scalar_tensor_tensor(out=hs, in0=ps[:], scalar=0.5, in1=hs, op0=ALU.mult, op1=ALU.add),
                                 [ps, H], [H])

        def proj(w_ap, rhs_buf, rhsv, evac, ntt=2, tw=512):
            for op_ in range(8):
                ws, wv = wload(w_ap, D, 0, 16, op_ * 256, 256)
                for ol in range(2):
                    oc = op_ * 2 + ol
                    for tt in range(ntt):
                        ps = k.psum()
                        for kc in range(16):
                            k.op("pe", lambda e: e.matmul(ps[:, 0:tw], lhsT=wv[:, kc, ol * 128:(ol + 1) * 128], rhs=rhsv[:, kc, tt * tw:(tt + 1) * tw],
                                                           start=(kc == 0), stop=(kc == 15)), [ws, rhs_buf], [ps], inc=(kc == 15))
                        evac(oc, tt, ps)

        o = [0]

        def al(nbytes, dt, pat=None, p1=128, **kw):
            v = av(o[0], nbytes, dt, pat, 0, p1, **kw)
            o[0] += nbytes
            return v

        P0 = k.view(None, "P0")
        CLin = al(5 * 2048, F32, "p (a n) -> p a n", a=5)
        k.dma("sp", CLin, cl_d.rearrange("a p n -> p a n"), P0, dIN)
        names = ["dt", "mag", "ang", "kf", "sn", "cs", "ar", "ai", "nr", "den", "fr", "fi", "t1", "t2", "pr", "pi", "bbr", "bbi", "t3", "t4"]
        cl = {n: al(2048, F32) for n in names}
        cli = al(2048, I32)
        WPc = al(32768, BF16, "p (q t r s) -> p q t r s", q=8, t=16, r=2)

        def abar(L, li_, lr_, ldt_, ki, eng="dve"):
            A = lambda fn: k.op("act", fn, [P0], [P0])
            V = lambda fn: k.op("dve", fn, [P0], [P0])
            A(lambda e: e.activation(out=L["dt"], in_=ldt_, func=AF.Exp))
            V(lambda e: e.tensor_tensor(out=L["mag"], in0=lr_, in1=L["dt"], op=ALU.mult))
            A(lambda e: e.activation(out=L["mag"], in_=L["mag"], func=AF.Exp))
            V(lambda e: e.tensor_tensor(out=L["ang"], in0=li_, in1=L["dt"], op=ALU.mult))
            V(lambda e: e.tensor_scalar(out=L["kf"], in0=L["ang"], scalar1=1.0 / TWO_PI, scalar2=None, op0=ALU.mult))
            V(lambda e: e.tensor_copy(out=ki, in_=L["kf"]))
            V(lambda e: e.tensor_copy(out=L["kf"], in_=ki))
            V(lambda e: e.scalar_tensor_tensor(out=L["ang"], in0=L["kf"], scalar=-TWO_PI, in1=L["ang"], op0=ALU.mult, op1=ALU.add))
            for nm, sh in (("sn", 0.0), ("cs", PI / 2)):
                V(lambda e: e.tensor_scalar(out=L["t1"], in0=L["ang"], scalar1=sh, scalar2=None, op0=ALU.add))
                V(lambda e: e.tensor_scalar(out=L["t2"], in0=L["t1"], scalar1=PI, scalar2=-TWO_PI, op0=ALU.is_gt, op1=ALU.mult))
                V(lambda e: e.tensor_tensor(out=L["t1"], in0=L["t1"], in1=L["t2"], op=ALU.add))
                V(lambda e: e.tensor_scalar(out=L["t2"], in0=L["t1"], scalar1=-PI, scalar2=TWO_PI, op0=ALU.is_lt, op1=ALU.mult))
                V(lambda e: e.tensor_tensor(out=L["t1"], in0=L["t1"], in1=L["t2"], op=ALU.add))
                V(lambda e: e.tensor_scalar(out=L["t1"], in0=L["t1"], scalar1=PI, scalar2=-PI, op0=ALU.min, op1=ALU.max))
                A(lambda e: e.activation(out=L[nm], in_=L["t1"], func=AF.Sin))
            V(lambda e: e.tensor_tensor(out=L["ar"], in0=L["mag"], in1=L["cs"], op=ALU.mult))
            V(lambda e: e.tensor_tensor(out=L["ai"], in0=L["mag"], in1=L["sn"], op=ALU.mult))
            V(lambda e: e.tensor_scalar(out=L["nr"], in0=L["ar"], scalar1=-1.0, scalar2=None, op0=ALU.add))
            V(lambda e: e.tensor_tensor(out=L["den"], in0=lr_, in1=lr_, op=ALU.mult))
            V(lambda e: e.tensor_tensor(out=L["t1"], in0=li_, in1=li_, op=ALU.mult))
            V(lambda e: e.tensor_tensor(out=L["den"], in0=L["den"], in1=L["t1"], op=ALU.add))
            V(lambda e: e.reciprocal(out=L["den"], in_=L["den"]))
            V(lambda e: e.tensor_tensor(out=L["t1"], in0=L["nr"], in1=lr_, op=ALU.mult))
            V(lambda e: e.tensor_tensor(out=L["t2"], in0=L["ai"], in1=li_, op=ALU.mult))
            V(lambda e: e.tensor_tensor(out=L["t1"], in0=L["t1"], in1=L["t2"], op=ALU.add))
            V(lambda e: e.tensor_tensor(out=L["fr"], in0=L["t1"], in1=L["den"], op=ALU.mult))
            V(lambda e: e.tensor_tensor(out=L["t1"], in0=L["ai"], in1=lr_, op=ALU.mult))
            V(lambda e: e.tensor_tensor(out=L["t2"], in0=L["nr"], in1=li_, op=ALU.mult))
            V(lambda e: e.tensor_tensor(out=L["t1"], in0=L["t1"], in1=L["t2"], op=ALU.subtract))
            V(lambda e: e.tensor_tensor(out=L["fi"], in0=L["t1"], in1=L["den"], op=ALU.mult))

        def cmul(L, outr, outi, ar_, ai_, br_, bi_, t1, t2):
            V = lambda fn: k.op("dve", fn, [P0], [P0])
            V(lambda e: e.tensor_tensor(out=t1, in0=ar_, in1=br_, op=ALU.mult))
            V(lambda e: e.tensor_tensor(out=t2, in0=ai_, in1=bi_, op=ALU.mult))
            V(lambda e: e.tensor_tensor(out=t1, in0=t1, in1=t2, op=ALU.subtract))
            V(lambda e: e.tensor_tensor(out=t2, in0=ar_, in1=bi_, op=ALU.mult))
            V(lambda e: e.tensor_tensor(out=outi, in0=ai_, in1=br_, op=ALU.mult))
            V(lambda e: e.tensor_tensor(out=outi, in0=outi, in1=t2, op=ALU.add))
            V(lambda e: e.tensor_copy(out=outr, in_=t1))

        V0 = lambda fn: k.op("dve", fn, [P0], [P0])
        ZT = al(16384, BF16)
        BZT = k.view(None, "ZT")
        k.op("dve", lambda e: e.memset(ZT, 0.0), [], [BZT])
        for q in range(8):
            for hh in range(2):
                k.dma("sp", WPd[q, :, hh * 8192:(hh + 1) * 8192], ZT, dWP, BZT)
        abar(cl, CLin[:, 1, :], CLin[:, 0, :], CLin[:, 2, :], cli)
        cmul(cl, cl["bbr"], cl["bbi"], cl["fr"], cl["fi"], CLin[:, 3, :], CLin[:, 4, :], cl["t1"], cl["t2"])
        V0(lambda e: e.memset(cl["pr"], 1.0))
        V0(lambda e: e.memset(cl["pi"], 0.0))
        for kk in range(16):
            tau = 15 - kk
            cmul(cl, cl["t3"], cl["t4"], cl["pr"], cl["pi"], cl["bbr"], cl["bbi"], cl["t1"], cl["t2"])
            V0(lambda e: e.tensor_copy(out=WPc[:, :, tau, 0, :], in_=cl["t3"].rearrange("p (q s) -> p q s", q=8)))
            V0(lambda e: e.tensor_copy(out=WPc[:, :, tau, 1, :], in_=cl["t4"].rearrange("p (q s) -> p q s", q=8)))
            if kk < 15:
                cmul(cl, cl["pr"], cl["pi"], cl["pr"], cl["pi"], cl["ar"], cl["ai"], cl["t1"], cl["t2"])
        k.barrier()
        WPdv = WPd.rearrange("q c (g x) -> q c g x", g=8)
        for gi in range(8):
            for q in range(8):
                k.dma("sp", WPdv[q, 16 * gi:16 * gi + 16, gi, :], WPc[16 * gi:16 * gi + 16, q].rearrange("p t r s -> p (t r s)"), dWP, P0)

        k.barrier()
        o[0] = 0
        SL3 = al(3 * 256, F32, "p (a n) -> p a n", a=3)
        k.dma("sp", SL3[0:64], sl3_d.rearrange("a p n -> p a n"), P0, dIN)
        SL4 = al(4 * 4096, F32, "p (a n) -> p a n", a=4)
        k.dma("sp", SL4[0:64], sl4_d.rearrange("a p n -> p a n"), P0, dIN)
        sl = {n: al(256, F32)[0:64] for n in names}
        sli = al(256, I32)[0:64]
        abar(sl, SL3[0:64, 1, :], SL3[0:64, 0, :], SL3[0:64, 2, :], sli)
        PWr = al(17 * 256, F32, "p (k g) -> p k g", k=17)[0:64]
        PWi = al(17 * 256, F32, "p (k g) -> p k g", k=17)[0:64]
        V0(lambda e: e.memset(PWr[:, 0, :], 1.0))
        V0(lambda e: e.memset(PWi[:, 0, :], 0.0))
        for kk in range(16):
            cmul(sl, PWr[:, kk + 1, :], PWi[:, kk + 1, :], PWr[:, kk, :], PWi[:, kk, :], sl["ar"], sl["ai"], sl["t1"], sl["t2"])
        AT = k.sb("AT", [64, 2, 64], F32)
        k.op("dve", lambda e: e.tensor_copy(out=AT[:, 0, :], in_=PWr[:, 16, :]), [P0], [AT])
        k.op("dve", lambda e: e.tensor_copy(out=AT[:, 1, :], in_=PWi[:, 16, :]), [P0], [AT])
        Cre = SL4[0:64, 0, :].rearrange("p (g h) -> p g h", h=16)
        Cim = SL4[0:64, 1, :].rearrange("p (g h) -> p g h", h=16)
        Bres = SL4[0:64, 2, :].rearrange("p (g h) -> p g h", h=16)
        Bims = SL4[0:64, 3, :].rearrange("p (g h) -> p g h", h=16)
        frb = sl["fr"].unsqueeze(2).broadcast_to([64, 64, 16])
        fib = sl["fi"].unsqueeze(2).broadcast_to([64, 64, 16])
        BBr = al(4096, F32, "p (g h) -> p g h", h=16)[0:64]
        BBi = al(4096, F32, "p (g h) -> p g h", h=16)[0:64]
        TA = al(4096, F32, "p (g h) -> p g h", h=16)[0:64]
        TB = al(4096, F32, "p (g h) -> p g h", h=16)[0:64]
        V0(lambda e: e.tensor_tensor(out=TA, in0=Bres, in1=frb, op=ALU.mult))
        V0(lambda e: e.tensor_tensor(out=TB, in0=Bims, in1=fib, op=ALU.mult))
        V0(lambda e: e.tensor_tensor(out=BBr, in0=TA, in1=TB, op=ALU.subtract))
        V0(lambda e: e.tensor_tensor(out=TA, in0=Bims, in1=frb, op=ALU.mult))
        V0(lambda e: e.tensor_tensor(out=TB, in0=Bres, in1=fib, op=ALU.mult))
        V0(lambda e: e.tensor_tensor(out=TA, in0=TA, in1=TB, op=ALU.add))
        V0(lambda e: e.tensor_scalar(out=BBi, in0=TA, scalar1=-1.0, scalar2=None, op0=ALU.mult))
        Dr = al(8 * 17 * 16 * 4, F32, "p (g k h) -> p g k h", g=8, k=17)[0:64]
        Di = al(8 * 17 * 16 * 4, F32, "p (g k h) -> p g k h", g=8, k=17)[0:64]
        T1 = al(8 * 17 * 16 * 4, F32, "p (g k h) -> p g k h", g=8, k=17)[0:64]
        T2 = al(8 * 17 * 16 * 4, F32, "p (g k h) -> p g k h", g=8, k=17)[0:64]
        BPr = al(8 * 128 * 4, F32, "p (g n) -> p g n", g=8)[0:64]
        BPi = al(8 * 128 * 4, F32, "p (g n) -> p g n", g=8)[0:64]
        V0(lambda e: e.memset(BPr, 0.0))
        V0(lambda e: e.memset(BPi, 0.0))
        WCts = [al(16384, BF16, "p (g t r n) -> p g t r n", g=8, t=4, r=2)[0:64] for _ in range(2)]
        BWCT = [k.view(None, "WCT0"), k.view(None, "WCT1")]
        for i_ in range(2):
            k.op("dve", lambda e: e.memset(WCts[i_], 0.0), [], [BWCT[i_]])
        wci = [0]
        Kc = al(1024, F32, "p (k h) -> p k h", k=16)
        KB = al(4096, BF16, "p (k g h) -> p k g h", k=16, g=8)
        assert o[0] <= 131072, o[0]
        for q in range(8):
            gs = slice(8 * q, 8 * q + 8)
            pwr = PWr[:, :, gs].rearrange("p k g -> p g k").unsqueeze(3).broadcast_to([64, 8, 17, 16])
            pwi = PWi[:, :, gs].rearrange("p k g -> p g k").unsqueeze(3).broadcast_to([64, 8, 17, 16])
            cre = Cre[:, gs, :].unsqueeze(2).broadcast_to([64, 8, 17, 16])
            cim = Cim[:, gs, :].unsqueeze(2).broadcast_to([64, 8, 17, 16])
            V0(lambda e: e.tensor_tensor(out=T1, in0=cre, in1=pwr, op=ALU.mult))
            V0(lambda e: e.tensor_tensor(out=T2, in0=cim, in1=pwi, op=ALU.mult))
            V0(lambda e: e.tensor_tensor(out=Dr, in0=T1, in1=T2, op=ALU.subtract))
            V0(lambda e: e.tensor_tensor(out=T1, in0=cre, in1=pwi, op=ALU.mult))
            V0(lambda e: e.tensor_tensor(out=T2, in0=cim, in1=pwr, op=ALU.mult))
            V0(lambda e: e.tensor_tensor(out=Di, in0=T1, in1=T2, op=ALU.add))
            for gi in range(8):
                V0(lambda e: e.tensor_copy(out=BPr[:, gi, 16 * gi:16 * gi + 16], in_=BBr[:, 8 * q + gi, :]))
                V0(lambda e: e.tensor_copy(out=BPi[:, gi, 16 * gi:16 * gi + 16], in_=BBi[:, 8 * q + gi, :]))
            ps = k.psum()
            n = 0
            for gi in range(8):
                for (bp, dd) in ((BPr, Dr), (BPi, Di)):
                    k.op("pe", lambda e: e.matmul(ps[:, 0:256], lhsT=bp[:, gi, :], rhs=dd[:, gi, 0:16, :], start=(n == 0), stop=(n == 15)),
                         [P0], [ps], inc=(n == 15))
                    n += 1
            k.op("act", lambda e: e.activation(out=Kc, in_=ps[:, 0:256].rearrange("p (k h) -> p k h", k=16), func=AF.Copy), [ps], [P0])
            for kk in range(16):
                V0(lambda e: e.tensor_tensor(out=KB[:, kk], in0=Kc[:, kk].unsqueeze(1).broadcast_to([128, 8, 16]),
                                             in1=mask8[:, :].unsqueeze(2).broadcast_to([128, 8, 16]), op=ALU.mult))
            k.dma("sp", KBd[q], KB.rearrange("p k g h -> p (k g h)"), dKB, P0)
            WCdq = WCd[q].rearrange("p (g t r n) -> p g t r n", g=8, t=16, r=2)
            for tq in range(4):
                wt = WCts[wci[0] % 2]; bw = BWCT[wci[0] % 2]; wci[0] += 1
                for gi in range(8):
                    k.op("dve", lambda e: e.tensor_copy(out=wt[:, gi, :, 0, 16 * gi:16 * gi + 16], in_=Dr[:, gi, 1 + 4 * tq:5 + 4 * tq, :]), [P0], [bw])
                    k.op("dve", lambda e: e.tensor_scalar(out=wt[:, gi, :, 1, 16 * gi:16 * gi + 16], in0=Di[:, gi, 1 + 4 * tq:5 + 4 * tq, :], scalar1=-1.0, scalar2=None, op0=ALU.mult),
                         [P0], [bw])
                k.dma("sp", WCdq[:, :, 4 * tq:4 * tq + 4], wt, dWC, bw)
        k.barrier()

        xv = xT.rearrange("(c p) t -> p c t", p=128)
        H1v = H1d.rearrange("(c p) t -> p c t", p=128)
        UPv = UPd.rearrange("(c p) t -> p c t", p=128)
        YLv = YLd.rearrange("(c p) t -> p c t", p=128)
        MRGv = MRGd.rearrange("(c p) t -> p c t", p=128)
        PSTv = PSTd.rearrange("p (g r c) -> p g r c", g=64, r=2)
        for s in range(2):
            t0 = s * ST
            for c4 in range(4):
                k.dma("sp", Hv[:, 4 * c4:4 * c4 + 4, :], xv[:, 4 * c4:4 * c4 + 4, t0:t0 + ST], H, dIN)
            rmsnorm(H, lambda kc, tt: Hv[:, kc, tt * 512:(tt + 1) * 512], XN, lambda kc, tt: XNv[:, kc, tt * 512:(tt + 1) * 512], 16, GC["ffn1"], 2, 2048.0)
            ffn(w1g, w1u, w1d)
            k.dma("sp", H1v[:, :, t0:t0 + ST], Hv, dH1, H)
            rmsnorm(H, lambda kc, tt: Hv[:, kc, tt * 512:(tt + 1) * 512], XN, lambda kc, tt: XNv[:, kc, tt * 512:(tt + 1) * 512], 16, GC["mix"], 2, 2048.0)

            def ev_in(oc, tt, ps):
                if oc < 8:
                    k.op("act", lambda e: e.activation(out=ABv[:, oc, tt * 512:(tt + 1) * 512], in_=ps[:], func=AF.Copy), [ps], [AB])
                else:
                    sb_ = rot(stb, "stb")
                    k.op("act", lambda e: e.activation(out=sb_[:], in_=ps[:], func=AF.Copy), [ps], [sb_])
                    k.dma("sp", UPv[:, oc - 8, t0 + tt * 512:t0 + (tt + 1) * 512], sb_[:], dUP, sb_)
            proj(w_in, XN, XNv, ev_in)
            for q in range(8):
                kbs = k.wslot()
                kbv = kbs[:, 0:2048].rearrange("p (k n) -> p k n", k=16)
                k.dma("sp", kbs[:, 0:2048], KBd[q], kbs, dKB)
                for tt in range(2):
                    ps = k.psum()
                    uu = ABv[:, q, tt * 512:(tt + 1) * 512].rearrange("p (c j) -> p c j", j=16)
                    pp = ps[:, :].rearrange("p (c j) -> p c j", j=16)
                    for kk in range(16):
                        k.op("pe", lambda e: e.matmul(pp[:, :, kk:16], lhsT=kbv[:, kk, :], rhs=uu[:, :, 0:16 - kk], start=(kk == 0), stop=(kk == 15)),
                             [kbs, AB], [ps], inc=(kk == 15))
                    yl = rot(stg, "stg")
                    k.op("dve", lambda e: e.scalar_tensor_tensor(out=yl[:], in0=ABv[:, q, tt * 512:(tt + 1) * 512], scalar=GV[:, GC["dskip"] + q:GC["dskip"] + q + 1],
                                                                  in1=ps[:], op0=ALU.mult, op1=ALU.add), [AB, ps, GV], [yl])
                    k.dma("sp", YLv[:, q, t0 + tt * 512:t0 + (tt + 1) * 512], yl[:], dYL, yl)
                uq = ABv[:, q, :].rearrange("p (c j) -> p c j", j=16)
                for gp in range(4):
                    wps = k.wslot()
                    wpv = wps[:, :].rearrange("p (g t r s) -> p g t r s", g=2, t=16, r=2)
                    k.dma("sp", wps[:, :], WPd[q, :, gp * 4096:(gp + 1) * 4096], wps, dWP)
                    ps = k.psum()
                    for gl in range(2):
                        for tau in range(16):
                            o_ = (gl * 16 + tau) * 128
                            k.op("pe", lambda e: e.matmul(ps[:, gl * 64:(gl + 1) * 64], lhsT=wps[:, o_:o_ + 128], rhs=uq[:, :, tau],
                                                           start=(tau == 0), stop=(tau == 15)), [wps, AB], [ps], inc=(tau == 15))
                    pst = rot(stg, "stg")
                    k.op("act", lambda e: e.activation(out=pst[:, 0:128], in_=ps[:, 0:128], func=AF.Copy), [ps], [pst])
                    g0 = 8 * q + 2 * gp
                    for ri in range(2):
                        k.dma("sp", PSTv[:, g0:g0 + 2, ri, s * 64:(s + 1) * 64], pst[64 * ri:64 * ri + 64, 0:128].rearrange("p (g c) -> p g c", g=2), dPST, pst)
        k.barrier()

        o[0] = 0
        SSb = al(64 * 2 * 129 * 2 + 0, BF16, "p (g r c) -> p g r c", g=64, r=2)[0:64]
        SS = al(64 * 2 * 129 * 4, F32, "p (g r c) -> p g r c", g=64, r=2)[0:64]
        PX = k.view(None, "PX")
        XS = k.sb("XS", [128, 256], F32)
        XR = k.sb("XR", [128, 256], F32)
        sct = [k.sb("sc%d" % i, [64, 64, 2], F32) for i in range(2)]
        A1 = k.sb("A1", [64, 64, 2], F32)
        A2 = k.sb("A2", [64, 64, 2], F32)
        k.op("dve", lambda e: e.tensor_copy(out=A1[:, :, 0], in_=AT[:, 0, :]), [AT], [A1])
        k.op("dve", lambda e: e.tensor_copy(out=A1[:, :, 1], in_=AT[:, 0, :]), [AT], [A1])
        k.op("dve", lambda e: e.tensor_scalar(out=A2[:, :, 0], in0=AT[:, 1, :], scalar1=-1.0, scalar2=None, op0=ALU.mult), [AT], [A2])
        k.op("dve", lambda e: e.tensor_copy(out=A2[:, :, 1], in_=AT[:, 1, :]), [AT], [A2])

        def scan():
            VX = lambda fn: k.op("dve", fn, [PX, A1, A2], [PX])
            for c in range(128):
                prev = SS[:, :, :, c]
                cur = SS[:, :, :, c + 1]
                t1, t2 = sct[0], sct[1]
                VX(lambda e: e.tensor_tensor(out=t1[:], in0=prev, in1=A1[:], op=ALU.mult))
                VX(lambda e: e.tensor_tensor(out=t2[:, :, 0], in0=SS[:, :, 1, c], in1=A2[:, :, 0], op=ALU.mult))
                VX(lambda e: e.tensor_tensor(out=t2[:, :, 1], in0=SS[:, :, 0, c], in1=A2[:, :, 1], op=ALU.mult))
                VX(lambda e: e.tensor_tensor(out=t1[:], in0=t1[:], in1=t2[:], op=ALU.add))
                VX(lambda e: e.tensor_tensor(out=cur, in0=cur, in1=t1[:], op=ALU.add))

        k.dma("sp", SS[:, :, :, 1:129], PSTv, PX, dPST)
        k.op("dve", lambda e: e.memset(SS[:, :, :, 0], 0.0), [PX], [PX])
        scan()
        k.op("dve", lambda e: e.memset(XS[:], 0.0), [], [XS])
        k.op("dve", lambda e: e.tensor_copy(out=XS[0:64, 0:128].rearrange("p (g r) -> p g r", r=2), in_=SS[:, :, :, 128]), [PX], [XS])
        hb = rot(stb, "stb")
        k.dma("sp", hb[:, 0:128].rearrange("p (q j) -> p q j", q=8), UPv[:, :, NT - 16:NT], hb, dUP)
        k.op("dve", lambda e: e.tensor_copy(out=XS[:, 128:256], in_=hb[:, 0:128]), [hb], [XS])
        k.dma("sp", XINd.ap(), XS[:], dXIN, XS)
        ccsem = es.enter_context(nc.semaphore("ccsem"))
        k._deps("pool", [dXIN], [dXOUT])
        nc.gpsimd.collective_compute("AllGather", ALU.bypass, replica_groups=[[0, 1], [2, 3], [4, 5], [6, 7]],
                                     ins=[XINd.ap().opt()], outs=[XOUTd.ap().opt()]).then_inc(ccsem)
        k.prog["pool"].append(("i", id(ccsem), 1))
        dXOUT.lw = (("d", ccsem), 1)
        dXOUT.rd = {}
        dXIN.rd[("d", ccsem)] = 1
        k.dma("sp", XR[:], XOUTd.ap()[0:128, :], XR, dXOUT)
        k.op("dve", lambda e: e.tensor_scalar(out=XR[:], in0=XR[:], scalar1=flag[:, 0:1], scalar2=None, op0=ALU.mult), [XR, flag], [XR])
        k.dma("sp", SS[:, :, :, 1:129], PSTv, PX, dPST)
        k.op("dve", lambda e: e.tensor_copy(out=SS[:, :, :, 0], in_=XR[0:64, 0:128].rearrange("p (g r) -> p g r", r=2)), [XR, PX], [PX])
        scan()
        k.op("dve", lambda e: e.tensor_copy(out=SSb[:, 0:32], in_=SS[:, 0:32]), [PX], [PX])
        k.op("act", lambda e: e.activation(out=SSb[:, 32:64], in_=SS[:, 32:64], func=AF.Copy), [PX], [PX])

        k.barrier()
        o[0] = 33024
        YGb = al(32768, BF16, "p (q t) -> p q t", q=8)
        WGL = al(16384, BF16, "p (c n) -> p c n", c=8)
        YLt = al(8192, F32)
        Yt = al(8192, F32)
        Tt = al(8192, F32)
        assert o[0] <= 131072, o[0]
        BYG = k.view(None, "YGb"); BWGL = k.view(None, "WGL"); BYL = k.view(None, "YLt"); BY = k.view(None, "Yt"); BT = k.view(None, "Tt")
        k.dma("pool", WGL, w_glu.rearrange("(c p) n -> p c n", p=128), BWGL, dIN)
        for q in range(8):
            k.dma("sp", YLt, YLv[:, q, :], BYL, dYL)
            pss = []
            for tq in range(4):
                ps = k.psum()
                wl = []
                for gh in range(2):
                    wcs = k.wslot()
                    wcv = wcs[0:64, :].rearrange("p (g t r n) -> p g t r n", g=4, t=4, r=2)
                    src = WCd[q].rearrange("p (g t r n) -> p g t r n", g=8, t=16, r=2)[:, 4 * gh:4 * gh + 4, 4 * tq:4 * tq + 4]
                    k.dma("sp", wcv, src, wcs, dWC)
                    wl.append((wcs, wcv))
                for tl in range(4):
                    n = 0
                    for gh in range(2):
                        wcs, wcv = wl[gh]
                        for gl in range(4):
                            gi = 4 * gh + gl
                            for ri in range(2):
                                k.op("pe", lambda e: e.matmul(ps[:, tl * 128:(tl + 1) * 128], lhsT=wcv[:, gl, tl, ri, :], rhs=SSb[:, 8 * q + gi, ri, 0:128],
                                                               start=(n == 0), stop=(n == 15)), [wcs, PX], [ps], inc=(n == 15))
                                n += 1
                pss.append(ps)
            for tq in range(4):
                ps = pss[tq]
                yv = Yt.rearrange("p (c j) -> p c j", j=16)[:, :, 4 * tq:4 * tq + 4]
                lv = YLt.rearrange("p (c j) -> p c j", j=16)[:, :, 4 * tq:4 * tq + 4]
                pv = ps[:, :].rearrange("p (j c) -> p c j", j=4)
                k.op("dve", lambda e: e.tensor_tensor(out=yv, in0=lv, in1=pv, op=ALU.add), [BYL, ps], [BY])
            k.op("act", lambda e: e.activation(out=Tt, in_=Yt, func=AF.Square), [BY], [BT])
            k.op("dve", lambda e: e.tensor_scalar(out=Tt, in0=Tt, scalar1=0.044715, scalar2=1.0, op0=ALU.mult, op1=ALU.add), [BT], [BT])
            k.op("dve", lambda e: e.tensor_tensor(out=Tt, in0=Tt, in1=Yt, op=ALU.mult), [BT, BY], [BT])
            k.op("act", lambda e: e.activation(out=Tt, in_=Tt, func=AF.Tanh, scale=0.7978845608028654), [BT], [BT])
            k.op("dve", lambda e: e.tensor_scalar(out=Tt, in0=Tt, scalar1=1.0, scalar2=0.5, op0=ALU.add, op1=ALU.mult), [BT], [BT])
            k.op("dve", lambda e: e.tensor_tensor(out=YGb[:, q, :], in0=Tt, in1=Yt, op=ALU.mult), [BT, BY], [BYG])
        o1 = o[0]
        YM = al(16384, F32, "p (c t) -> p c t", c=8)
        BYM = k.view(None, "YM")
        for tt in range(4):
            for oc in range(8):
                ps = k.psum()
                for kc in range(8):
                    k.op("pe", lambda e: e.matmul(ps[:], lhsT=WGL[:, kc, oc * 128:(oc + 1) * 128], rhs=YGb[:, kc, tt * 512:(tt + 1) * 512],
                                                   start=(kc == 0), stop=(kc == 7)), [BWGL, BYG], [ps], inc=(kc == 7))
                sg = rot(stg, "stg")
                k.op("act", lambda e: e.activation(out=sg[:], in_=ps[:], func=AF.Sigmoid, bias=GV[:, GC["bglu"] + oc:GC["bglu"] + oc + 1]), [ps, GV], [sg])
                k.op("dve", lambda e: e.tensor_tensor(out=YM[:, oc, :], in0=YGb[:, oc, tt * 512:(tt + 1) * 512], in1=sg[:], op=ALU.mult), [BYG, sg], [BYM])
            mt = {}

            def dstf(kc, t_):
                b = rot(stb, "stb")
                mt[kc] = b
                return b[:]
            ps = k.psum()
            for kc in range(8):
                sq = rot(sqt, "sq")
                k.op("act", lambda e: e.activation(out=sq[:], in_=YM[:, kc, :], func=AF.Square), [BYM], [sq])
                k.op("pe", lambda e: e.matmul(ps[:], lhsT=ones[:], rhs=sq[:], start=(kc == 0), stop=(kc == 7)), [ones, sq], [ps], inc=True)
            rs = rot(rst, "rs")
            k.op("act", lambda e: e.activation(out=rs[:], in_=ps[:], func=AF.Sqrt, bias=EPS[:, 0:1], scale=1.0 / 1024.0), [ps, EPS], [rs])
            k.op("dve", lambda e: e.reciprocal(out=rs[:], in_=rs[:]), [rs], [rs])
            for kc in range(8):
                b = rot(stb, "stb")
                k.op("dve", lambda e: e.scalar_tensor_tensor(out=b[:], in0=YM[:, kc, :], scalar=GV[:, GC["ossm"] + kc:GC["ossm"] + kc + 1], in1=rs[:],
                                                              op0=ALU.mult, op1=ALU.mult), [BYM, rs, GV], [b])
                k.dma("sp", MRGv[:, kc, tt * 512:(tt + 1) * 512], b[:], dMRG, b)
        k.barrier()
        o[0] = 0
        PL = al(32768, BF16, "p (c t) -> p c t", c=8)
        Vt = al(2064 * 4, F32)
        Sa = al(2064 * 4, F32)
        Sb_ = al(2064 * 4, F32)
        Ub = al(4096, BF16)
        ZT2 = al(16384, F32, "p (c t) -> p c t", c=8)
        assert o[0] <= 131072
        BPL = k.view(None, "PL"); BV = k.view(None, "Vt"); BSa = k.view(None, "Sa"); BSb = k.view(None, "Sb"); BU = k.view(None, "Ub"); BZ = k.view(None, "ZT2")
        WIN = (2, 4, 8, 16)
        XRh = XR[:, 128:256].rearrange("p (q j) -> p q j", q=8)
        for pc in range(8):
            gi = pc // 2
            w = WIN[gi]
            k.dma("sp", Ub, UPv[:, pc, :], BU, dUP)
            k.op("act", lambda e: e.activation(out=Vt[:, 16:2064], in_=Ub, func=AF.Copy), [BU], [BV])
            k.op("dve", lambda e: e.tensor_copy(out=Vt[:, 0:16], in_=XRh[:, pc, :]), [XR], [BV])
            cur, curb = Vt, BV
            sh = 1
            bufs2 = [(Sa, BSa), (Sb_, BSb)]
            bi = 0
            while sh < w:
                nxt, nxtb = bufs2[bi]
                bi ^= 1
                k.op("dve", lambda e: e.tensor_tensor(out=nxt[:, sh:2064], in0=cur[:, sh:2064], in1=cur[:, 0:2064 - sh], op=ALU.add), [curb], [nxtb])
                cur, curb = nxt, nxtb
                sh *= 2
            k.op("dve", lambda e: e.scalar_tensor_tensor(out=PL[:, pc, :], in0=cur[:, 16:2064], scalar=1.0 / w, in1=Vt[:, 16:2064], op0=ALU.mult, op1=ALU.subtract),
                 [curb, BV], [BPL])
            nxt, nxtb = bufs2[bi]
            k.op("dve", lambda e: e.tensor_tensor(out=nxt[:, 0:16], in0=cur[:, 16:32], in1=icnt[:, 16 * gi:16 * gi + 16], op=ALU.mult), [curb, icnt], [nxtb])
            k.op("dve", lambda e: e.tensor_tensor(out=PL[:, pc, 0:16], in0=nxt[:, 0:16], in1=Vt[:, 16:32], op=ALU.subtract), [nxtb, BV], [BPL])
        wpl = []
        for gi in range(4):
            ws = k.wslot()
            v = ws[:, 0:512].rearrange("p (c n) -> p c n", c=2)
            k.dma("pool", v, w_pool[gi].rearrange("(c p) n -> p c n", p=128), ws, dIN)
            wpl.append((ws, v))
        for tt in range(4):
            for gi in range(4):
                ws, v = wpl[gi]
                for dc in range(2):
                    ps = k.psum()
                    for cc in range(2):
                        k.op("pe", lambda e: e.matmul(ps[:], lhsT=v[:, cc, dc * 128:(dc + 1) * 128], rhs=PL[:, 2 * gi + cc, tt * 512:(tt + 1) * 512],
                                                       start=(cc == 0), stop=(cc == 1)), [ws, BPL], [ps], inc=(cc == 1))
                    oc = 2 * gi + dc
                    k.op("act", lambda e: e.activation(out=ZT2[:, oc, :], in_=ps[:], func=AF.Copy, scale=GV[:, GC["pscale"] + oc:GC["pscale"] + oc + 1]), [ps, GV], [BZ])
            ps = k.psum()
            for kc in range(8):
                sq = rot(sqt, "sq")
                k.op("act", lambda e: e.activation(out=sq[:], in_=ZT2[:, kc, :], func=AF.Square), [BZ], [sq])
                k.op("pe", lambda e: e.matmul(ps[:], lhsT=ones[:], rhs=sq[:], start=(kc == 0), stop=(kc == 7)), [ones, sq], [ps], inc=True)
            rs = rot(rst, "rs")
            k.op("act", lambda e: e.activation(out=rs[:], in_=ps[:], func=AF.Sqrt, bias=EPS[:, 0:1], scale=1.0 / 1024.0), [ps, EPS], [rs])
            k.op("dve", lambda e: e.reciprocal(out=rs[:], in_=rs[:]), [rs], [rs])
            for kc in range(8):
                b = rot(stb, "stb")
                k.op("dve", lambda e: e.scalar_tensor_tensor(out=b[:], in0=ZT2[:, kc, :], scalar=GV[:, GC["opool"] + kc:GC["opool"] + kc + 1], in1=rs[:],
                                                              op0=ALU.mult, op1=ALU.mult), [BZ, rs, GV], [b])
                k.dma("sp", MRGv[:, 8 + kc, tt * 512:(tt + 1) * 512], b[:], dMRG, b)
        k.barrier()

        MN = k.view(av(98304, 8192, BF16, "p (c m) -> p c m", c=16), "MN")
        KT = k.sb("KT", [128, 16, 256], BF16)
        VM = k.sb("VM", [128, 2, 2048], BF16)
        MF = Hv[:, :, 0:256]
        k.dma("sp", MF, memT.rearrange("(c p) m -> p c m", p=128), H, dIN)
        rmsnorm(H, lambda kc, tt: Hv[:, kc, 0:256], MN, lambda kc, tt: MN[:, kc, :], 16, GC["mem"], 1, 2048.0, tw=256)

        def ev_k(oc, tt, ps):
            k.op("act", lambda e: e.activation(out=KT[:, oc, :], in_=ps[:, 0:256], func=AF.Copy), [ps], [KT])
        proj(w_k, MN, MN, ev_k, ntt=1, tw=256)
        for op_ in range(8):
            ws, wv = wload(w_v, D, 0, 16, op_ * 256, 256)
            for mc in range(2):
                ps = k.psum()
                for kc in range(16):
                    k.op("pe", lambda e: e.matmul(ps[:, 0:256], lhsT=MN[:, kc, mc * 128:(mc + 1) * 128], rhs=wv[:, kc, :], start=(kc == 0), stop=(kc == 15)),
                         [ws, MN], [ps], inc=(kc == 15))
                k.op("act", lambda e: e.activation(out=VM[:, mc, op_ * 256:(op_ + 1) * 256], in_=ps[:, 0:256], func=AF.Copy), [ps], [VM])

        yv_ = yT.rearrange("(c p) t -> p c t", p=128)
        Et = [k.sb("Et%d" % i, [128, 2, 512], BF16) for i in range(2)]
        ecnt = [0]
        for s in range(2):
            t0 = s * ST
            k.dma("sp", Hv, H1v[:, :, t0:t0 + ST], H, dH1)
            k.dma("sp", XNv, MRGv[:, :, t0:t0 + ST], XN, dMRG)

            def ev_add(oc, tt, ps):
                hs = Hv[:, oc, tt * 512:(tt + 1) * 512]
                k.op("dve", lambda e: e.tensor_tensor(out=hs, in0=hs, in1=ps[:], op=ALU.add), [H, ps], [H])
            proj(w_out, XN, XNv, ev_add)
            rmsnorm(H, lambda kc, tt: Hv[:, kc, tt * 512:(tt + 1) * 512], XN, lambda kc, tt: XNv[:, kc, tt * 512:(tt + 1) * 512], 16, GC["xattn"], 2, 2048.0)

            def ev_q(oc, tt, ps):
                k.op("act", lambda e: e.activation(out=ABv[:, oc, tt * 512:(tt + 1) * 512], in_=ps[:], func=AF.Copy, scale=512.0 ** -0.5), [ps], [AB])
            proj(w_q, XN, XNv, ev_q)
            for hh in range(4):
                for tt in range(2):
                    et = Et[ecnt[0] % 2]; ecnt[0] += 1
                    for mc in range(2):
                        ps = k.psum()
                        for dc in range(4):
                            k.op("pe", lambda e: e.matmul(ps[:], lhsT=KT[:, hh * 4 + dc, mc * 128:(mc + 1) * 128], rhs=ABv[:, hh * 4 + dc, tt * 512:(tt + 1) * 512],
                                                           start=(dc == 0), stop=(dc == 3)), [KT, AB], [ps], inc=(dc == 3))
                        k.op("act", lambda e: e.activation(out=et[:, mc, :], in_=ps[:], func=AF.Exp), [ps], [et])
                    ps2 = k.psum()
                    for mc in range(2):
                        k.op("pe", lambda e: e.matmul(ps2[:], lhsT=ones[:], rhs=et[:, mc, :], start=(mc == 0), stop=(mc == 1)), [ones, et], [ps2], inc=(mc == 1))
                    rs = rot(rst, "rs")
                    k.op("dve", lambda e: e.reciprocal(out=rs[:], in_=ps2[:]), [ps2], [rs])
                    for dvc in range(4):
                        ps3 = k.psum()
                        for mc in range(2):
                            k.op("pe", lambda e: e.matmul(ps3[:], lhsT=VM[:, mc, hh * 512 + dvc * 128:hh * 512 + (dvc + 1) * 128], rhs=et[:, mc, :],
                                                           start=(mc == 0), stop=(mc == 1)), [VM, et], [ps3], inc=(mc == 1))
                        k.op("dve", lambda e: e.tensor_tensor(out=XNv[:, hh * 4 + dvc, tt * 512:(tt + 1) * 512], in0=ps3[:], in1=rs[:], op=ALU.mult), [ps3, rs], [XN])
            proj(w_o, XN, XNv, ev_add)
            rmsnorm(H, lambda kc, tt: Hv[:, kc, tt * 512:(tt + 1) * 512], XN, lambda kc, tt: XNv[:, kc, tt * 512:(tt + 1) * 512], 16, GC["ffn2"], 2, 2048.0)
            ffn(w2g, w2u, w2d)
            for tt in range(2):
                ps = k.psum()
                for kc in range(16):
                    sq = rot(sqt, "sq")
                    k.op("act", lambda e: e.activation(out=sq[:], in_=Hv[:, kc, tt * 512:(tt + 1) * 512], func=AF.Square), [H], [sq])
                    k.op("pe", lambda e: e.matmul(ps[:], lhsT=ones[:], rhs=sq[:], start=(kc == 0), stop=(kc == 15)), [ones, sq], [ps], inc=True)
                rs = rot(rst, "rs")
                k.op("act", lambda e: e.activation(out=rs[:], in_=ps[:], func=AF.Sqrt, bias=EPS[:, 0:1], scale=1.0 / 2048.0), [ps, EPS], [rs])
                k.op("dve", lambda e: e.reciprocal(out=rs[:], in_=rs[:]), [rs], [rs])
                for kc in range(16):
                    b = rot(stg, "stg")
                    k.op("dve", lambda e: e.scalar_tensor_tensor(out=b[:], in0=Hv[:, kc, tt * 512:(tt + 1) * 512], scalar=GV[:, GC["final"] + kc:GC["final"] + kc + 1],
                                                                  in1=rs[:], op0=ALU.mult, op1=ALU.mult), [H, rs, GV], [b])
                    k.dma("sp", yv_[:, kc, t0 + tt * 512:t0 + (tt + 1) * 512], b[:], dY, b)
        k.finish([dY, dDBG])
        build_nc.sim = k.simulate()
    return nc


def _vec(v):
    v = np.asarray(v, np.float32).reshape(-1, 128)
    return np.ascontiguousarray(v.T)


def prep_inputs(inp):
    L = 0
    f = lambda a: np.ascontiguousarray(np.asarray(a, np.float32))
    x = f(inp["x"]); mem = f(inp["mem"])
    gv = np.concatenate([
        _vec(inp["g_ffn1"][L]), _vec(inp["g_mix"][L]), _vec(inp["g_xattn"][L]), _vec(inp["g_mem"][L]),
        _vec(inp["g_ffn2"][L]), _vec(inp["g_final"]), _vec(inp["g_out_ssm"][L]), _vec(inp["g_out_pool"][L]),
        _vec(inp["ssm_d"][L]), _vec(inp["b_glu"][L]), _vec(inp["pool_scale"][L])], axis=1)
    gv = np.ascontiguousarray(gv, np.float32)
    assert gv.shape == (128, 136)
    a_re = f(inp["ssm_a_re"][L]); a_im = f(inp["ssm_a_im"][L]); ldt = f(inp["ssm_log_dt"][L])
    b_re = f(inp["ssm_b_re"][L]); b_im = f(inp["ssm_b_im"][L])
    c_re = f(inp["ssm_c_re"][L]); c_im = f(inp["ssm_c_im"][L])

    def cl_gp(a):
        a4 = a.reshape(8, 8, 64)
        r = np.broadcast_to(a4[:, :, None, :], (8, 8, 16, 64))
        return np.ascontiguousarray(r.transpose(1, 2, 0, 3).reshape(128, 512))

    def cl_b(b):
        b5 = b.reshape(8, 8, 64, 16)
        return np.ascontiguousarray(b5.transpose(1, 3, 0, 2).reshape(128, 512))
    ldt_gp = np.broadcast_to(ldt[:, None], (64, 64))
    ssm_cl = np.stack([cl_gp(a_re), cl_gp(a_im), cl_gp(ldt_gp), cl_b(b_re), cl_b(b_im)]).astype(np.float32)
    ssm_sl3 = np.stack([a_re.T, a_im.T, ldt_gp.T]).astype(np.float32)
    ssm_sl3 = np.ascontiguousarray(ssm_sl3)
    sl_c = lambda c_: np.ascontiguousarray(c_.transpose(2, 0, 1).reshape(64, 1024))
    sl_b = lambda b_: np.ascontiguousarray(b_.transpose(1, 0, 2).reshape(64, 1024))
    ssm_sl4 = np.stack([sl_c(c_re), sl_c(c_im), sl_b(b_re), sl_b(b_im)]).astype(np.float32)
    mask8 = np.zeros((128, 8), np.float32)
    for c in range(128):
        mask8[c, c // 16] = 1.0
    shared = {
        "w1_gate": f(inp["w1_gate"][L]), "w1_up": f(inp["w1_up"][L]), "w1_down": f(inp["w1_down"][L]),
        "w2_gate": f(inp["w2_gate"][L]), "w2_up": f(inp["w2_up"][L]), "w2_down": f(inp["w2_down"][L]),
        "w_in": f(inp["w_in"][L]), "w_out": f(inp["w_out"][L]), "w_q": f(inp["w_q"][L]), "w_k": f(inp["w_k"][L]),
        "w_v": f(inp["w_v"][L]), "w_o": f(inp["w_o"][L]), "w_glu": f(inp["w_glu"][L]), "w_pool": f(inp["w_pool"][L]),
        "gv": gv, "ssm_cl": ssm_cl, "ssm_sl3": ssm_sl3, "ssm_sl4": ssm_sl4, "mask8": mask8,
    }
    in_maps = []
    for c in range(8):
        b, half = c // 2, c % 2
        m = dict(shared)
        m["xT"] = np.ascontiguousarray(x[b, half * NT:(half + 1) * NT, :].T)
        m["memT"] = np.ascontiguousarray(mem[b].T)
        m["flag"] = np.full((128, 1), float(half), np.float32)
        ic = np.zeros((128, 4, 16), np.float32)
        for gi, w in enumerate((2, 4, 8, 16)):
            for t in range(16):
                ic[:, gi, t] = 1.0 / (min(t + 1, w) if half == 0 else w)
        m["invcnt"] = ic.reshape(128, 64)
        in_maps.append(m)
    return in_maps


_NC = {}


def kernel(**inputs):
    in_maps = prep_inputs(inputs)
    if "nc" not in _NC:
        _NC["nc"] = build_nc()
    res = run_bass_kernel_spmd(_NC["nc"], in_maps, core_ids=list(range(8)))
    out = np.empty((4, 4096, D), np.float32)
    for c in range(8):
        b, half = c // 2, c % 2
        out[b, half * NT:(half + 1) * NT, :] = res.results[c]["yT"].T
    return out
```

```python
import numpy as np
from contextlib import ExitStack
import concourse.bass as bass
import concourse.mybir as mybir
from concourse.bass_utils import run_bass_kernel_spmd

F32 = mybir.dt.float32
BF16 = mybir.dt.bfloat16
I32 = mybir.dt.int32
ALU = mybir.AluOpType
AF = mybir.ActivationFunctionType

NT = 2048
ST = 1024
D = 2048
DFF = 5632
NFC = 44
TWO_PI = 6.283185307179586
PI = 3.141592653589793


class Buf:
    __slots__ = ("t", "name", "lw", "rd", "dsem", "dcnt")

    def __init__(self, t, name):
        self.t = t
        self.name = name
        self.lw = None
        self.rd = {}
        self.dsem = None
        self.dcnt = 0

    def __getitem__(self, idx):
        return self.t[idx]


class K:
    def __init__(self, nc, es):
        self.nc = nc
        self.es = es
        self.eng = {"pe": nc.tensor, "act": nc.scalar, "dve": nc.vector,
                    "pool": nc.gpsimd, "sp": nc.sync}
        self.sem = {}
        self.cnt = {}
        for k in self.eng:
            self.sem[k] = es.enter_context(nc.semaphore("s_" + k))
            self.cnt[k] = 0
        self.waited = {}
        self.prog = {k_: [] for k_ in self.eng}
        self.dbufs = []
        self.pe_open = False
        self.psb = []
        self.psi = 0
        self.wsb = []
        self.wsi = 0

    def sb(self, name, shape, dt):
        t = self.es.enter_context(self.nc.sbuf_tensor(name, shape, dt))
        return Buf(t, name)

    def view(self, t, name):
        return Buf(t, name)

    def psum(self):
        b = self.psb[self.psi % len(self.psb)]
        self.psi += 1
        return b

    def wslot(self):
        b = self.wsb[self.wsi % len(self.wsb)]
        self.wsi += 1
        return b

    def _wait(self, e, dep):
        key, c = dep
        if key == "pe" and e == "pe":
            return
        w = self.waited.get((e, key), 0)
        if w >= c:
            return
        sem = self.sem[key] if isinstance(key, str) else key[1]
        self.eng[e].wait_ge(sem, c)
        self.prog[e].append(("w", id(sem), c))
        self.waited[(e, key)] = c

    def _deps(self, e, reads, writes, dma_key=None):
        for b in reads:
            if b.lw is not None:
                self._wait(e, b.lw)
        for b in writes:
            if b.lw is not None and b.lw[0] != dma_key:
                self._wait(e, b.lw)
            for r in list(b.rd.items()):
                self._wait(e, r)

    def op(self, e, fn, reads=(), writes=(), inc=True):
        self._deps(e, reads, writes)
        ins = fn(self.eng[e])
        if inc:
            self.cnt[e] += 1
            ins.then_inc(self.sem[e], 1)
            self.prog[e].append(("i", id(self.sem[e]), 1))
            c = self.cnt[e]
            if e == "pe":
                self.pe_open = False
        else:
            assert e == "pe"
            c = self.cnt[e] + 1
            self.pe_open = True
        for b in writes:
            b.lw = (e, c)
            b.rd = {}
        for b in reads:
            b.rd[e] = c
        return ins

    def dma(self, e, out_ap, in_ap, dst, src, **kw):
        if dst.dsem is None:
            dst.dsem = self.es.enter_context(self.nc.semaphore("d_" + dst.name))
            self.dbufs.append(dst)
        key = ("d", dst.dsem)
        self._deps(e, [src] if src is not None else [], [dst], dma_key=key)
        ins = self.eng[e].dma_start(out=out_ap, in_=in_ap, **kw)
        ins.then_inc(dst.dsem, 16)
        self.prog[e].append(("i", id(dst.dsem), 16))
        dst.dcnt += 16
        if dst.lw is None or dst.lw[0] != key:
            dst.rd = {}
        dst.lw = (key, dst.dcnt)
        if src is not None:
            src.rd[key] = dst.dcnt
        return ins

    def barrier(self):
        assert not self.pe_open
        for e in self.eng:
            for k2 in self.eng:
                if k2 != e and self.cnt[k2] > 0:
                    self._wait(e, (k2, self.cnt[k2]))
            for b in self.dbufs:
                if b.dcnt > 0:
                    self._wait(e, (("d", b.dsem), b.dcnt))

    def simulate(self):
        pc = {e: 0 for e in self.prog}
        val = {}
        progress = True
        while progress:
            progress = False
            for e, pr in self.prog.items():
                while pc[e] < len(pr):
                    kind, sid, v = pr[pc[e]]
                    if kind == "w":
                        if val.get(sid, 0) >= v:
                            pc[e] += 1
                            progress = True
                        else:
                            break
                    else:
                        val[sid] = val.get(sid, 0) + v
                        pc[e] += 1
                        progress = True
        stuck = {e: (pc[e], len(pr)) for e, pr in self.prog.items() if pc[e] < len(pr)}
        return stuck, {e: len(pr) for e, pr in self.prog.items()}, dict(self.cnt)

    def finish(self, bufs):
        for b in bufs:
            if b.lw is not None:
                self._wait("sp", b.lw)


def build_nc(dbg=()):
    nc = bass.Bass("TRN2", target_bir_lowering=False)

    def din(name, shape, dt=F32):
        return nc.dram_tensor(name, shape, dt, kind="ExternalInput").ap()

    xT = din("xT", [D, NT])
    memT = din("memT", [D, 256])
    w1g = din("w1_gate", [D, DFF]); w1u = din("w1_up", [D, DFF]); w1d = din("w1_down", [DFF, D])
    w2g = din("w2_gate", [D, DFF]); w2u = din("w2_up", [D, DFF]); w2d = din("w2_down", [DFF, D])
    w_in = din("w_in", [D, D]); w_out = din("w_out", [D, D])
    w_q = din("w_q", [D, D]); w_k = din("w_k", [D, D]); w_v = din("w_v", [D, D]); w_o = din("w_o", [D, D])
    w_glu = din("w_glu", [1024, 1024]); w_pool = din("w_pool", [4, 256, 256])
    gv_d = din("gv", [128, 136])
    cl_d = din("ssm_cl", [5, 128, 512])
    sl3_d = din("ssm_sl3", [3, 64, 64])
    sl4_d = din("ssm_sl4", [4, 64, 1024])
    mask_d = din("mask8", [128, 8])
    flag_d = din("flag", [128, 1])
    icnt_d = din("invcnt", [128, 64])
    yT = nc.dram_tensor("yT", [D, NT], F32, kind="ExternalOutput").ap()

    H1d = nc.dram_tensor("H1d", [D, NT], F32).ap()
    UPd = nc.dram_tensor("UPd", [1024, NT], BF16).ap()
    YLd = nc.dram_tensor("YLd", [1024, NT], F32).ap()
    MRGd = nc.dram_tensor("MRGd", [D, NT], BF16).ap()
    WPd = nc.dram_tensor("WPd", [8, 128, 16384], BF16).ap()
    WCd = nc.dram_tensor("WCd", [8, 64, 32768], BF16).ap()
    KBd = nc.dram_tensor("KBd", [8, 128, 2048], BF16).ap()
    PSTd = nc.dram_tensor("PSTd", [64, 64 * 2 * 128], F32).ap()
    XINd = nc.dram_tensor("XINd", [128, 256], F32)
    XOUTd = nc.dram_tensor("XOUTd", [256, 256], F32)

    dbg_out = {}
    for name, shape in dbg:
        dbg_out[name] = nc.dram_tensor("dbg_" + name, shape, F32, kind="ExternalOutput").ap()

    with ExitStack() as es:
        k = K(nc, es)
        arena = es.enter_context(nc.sbuf_tensor("arena", [128, 65536], BF16))
        k.psb = [Buf(es.enter_context(nc.psum_tensor("ps%d" % i, [128, 512], F32)), "ps%d" % i) for i in range(8)]
        k.wsb = [k.sb("ws%d" % i, [128, 4096], BF16) for i in range(4)]
        GV = k.sb("GV", [128, 136], F32)
        ones = k.sb("ones", [128, 128], BF16)
        mask8 = k.sb("mask8s", [128, 8], F32)
        flag = k.sb("flags", [128, 1], F32)
        icnt = k.sb("icnt", [128, 64], F32)
        sqt = [k.sb("sq%d" % i, [128, 512], BF16) for i in range(4)]
        rst = [k.sb("rs%d" % i, [128, 512], F32) for i in range(2)]
        stg = [k.sb("stg%d" % i, [128, 512], F32) for i in range(4)]
        stb = [k.sb("stb%d" % i, [128, 512], BF16) for i in range(4)]
        cnts = {"sq": 0, "rs": 0, "stg": 0, "stb": 0}

        def rot(lst, key):
            b = lst[cnts[key] % len(lst)]
            cnts[key] += 1
            return b

        dH1 = k.view(H1d, "H1d"); dUP = k.view(UPd, "UPd"); dYL = k.view(YLd, "YLd"); dMRG = k.view(MRGd, "MRGd")
        dWP = k.view(WPd, "WPd"); dWC = k.view(WCd, "WCd"); dKB = k.view(KBd, "KBd"); dPST = k.view(PSTd, "PSTd")
        dXIN = k.view(XINd, "XINd"); dXOUT = k.view(XOUTd, "XOUTd"); dY = k.view(yT, "yT")
        dIN = k.view(xT, "inputs")
        dDBG = k.view(None, "dbg")

        GC = {"ffn1": 0, "mix": 16, "xattn": 32, "mem": 48, "ffn2": 64, "final": 80,
              "ossm": 96, "opool": 104, "dskip": 112, "bglu": 120, "pscale": 128}

        k.dma("sp", GV[:], gv_d, GV, dIN)
        k.dma("sp", mask8[:], mask_d, mask8, dIN)
        k.dma("sp", flag[:], flag_d, flag, dIN)
        k.dma("sp", icnt[:], icnt_d, icnt, dIN)
        k.op("dve", lambda e: e.memset(ones[:], 1.0), [], [ones])

        def av(off, nbytes, dt, pat=None, p0=0, p1=128, **kw):
            esz = 4 if dt in (F32, I32) else 2
            v = arena[p0:p1, off // 2:(off + nbytes) // 2]
            if dt != BF16:
                v = v.bitcast(dt)
            if pat:
                v = v.rearrange(pat, **kw)
            return v

        Hv = av(0, 65536, F32, "p (c t) -> p c t", c=16)
        XNv = av(65536, 32768, BF16, "p (c t) -> p c t", c=16)
        ABv = av(98304, 32768, BF16, "p (c t) -> p c t", c=16)
        H = k.view(Hv, "H"); XN = k.view(XNv, "XN"); AB = k.view(ABv, "AB")

        def dump(name, ap, buf):
            if name in dbg_out:
                k.dma("sp", dbg_out[name], ap, dDBG, buf)

        def rmsnorm(src_buf, src, dst_buf, dst, nk, gcol, ntt, dn, tw=512):
            for tt in range(ntt):
                ps = k.psum()
                for kc in range(nk):
                    sq = rot(sqt, "sq")
                    k.op("act", lambda e: e.activation(out=sq[:, 0:tw], in_=src(kc, tt), func=AF.Square), [src_buf], [sq])
                    k.op("pe", lambda e: e.matmul(ps[:, 0:tw], lhsT=ones[:], rhs=sq[:, 0:tw], start=(kc == 0), stop=(kc == nk - 1)),
                         [ones, sq], [ps], inc=True)
                rs = rot(rst, "rs")
                k.op("act", lambda e: e.activation(out=rs[:, 0:tw], in_=ps[:, 0:tw], func=AF.Sqrt, bias=EPSB[:, 0:1], scale=1.0 / dn), [ps, EPS], [rs])
                k.op("dve", lambda e: e.reciprocal(out=rs[:, 0:tw], in_=rs[:, 0:tw]), [rs], [rs])
                for kc in range(nk):
                    k.op("dve", lambda e: e.scalar_tensor_tensor(out=dst(kc, tt), in0=src(kc, tt), scalar=GV[:, gcol + kc:gcol + kc + 1],
                                                                  in1=rs[:, 0:tw], op0=ALU.mult, op1=ALU.mult), [src_buf, rs, GV], [dst_buf])

        EPS = k.sb("eps", [128, 1], F32)
        EPSB = EPS
        k.op("dve", lambda e: e.memset(EPS[:], 1e-6), [], [EPS])

        def wload(w_ap, rows, r0, nkc, c0, ncols, name="w"):
            ws = k.wslot()
            v = ws[:, 0:nkc * ncols].rearrange("p (c n) -> p c n", c=nkc)
            src = w_ap[r0:r0 + nkc * 128, c0:c0 + ncols].rearrange("(c p) n -> p c n", p=128)
            k.dma("pool", v, src, ws, dIN)
            return ws, v

        def ffn(wg, wu, wd):
            parts = [(0, 16), (16, 32), (32, 44)]
            for (f0, f1) in parts:
                nf = f1 - f0
                for fp in range(f0, f1, 2):
                    wgs, wgv = wload(wg, D, 0, 16, fp * 128, 256)
                    wus, wuv = wload(wu, D, 0, 16, fp * 128, 256)
                    for fl in range(2):
                        f = fp + fl
                        for tt in range(2):
                            pg = k.psum(); pu = k.psum()
                            for kc in range(16):
                                k.op("pe", lambda e: e.matmul(pg[:], lhsT=wgv[:, kc, fl * 128:(fl + 1) * 128], rhs=XNv[:, kc, tt * 512:(tt + 1) * 512],
                                                               start=(kc == 0), stop=(kc == 15)), [wgs, XN], [pg], inc=(kc == 15))
                            for kc in range(16):
                                k.op("pe", lambda e: e.matmul(pu[:], lhsT=wuv[:, kc, fl * 128:(fl + 1) * 128], rhs=XNv[:, kc, tt * 512:(tt + 1) * 512],
                                                               start=(kc == 0), stop=(kc == 15)), [wus, XN], [pu], inc=(kc == 15))
                            sg = rot(stb, "stb")
                            k.op("act", lambda e: e.activation(out=sg[:], in_=pg[:], func=AF.Silu), [pg], [sg])
                            k.op("dve", lambda e: e.tensor_tensor(out=ABv[:, f - f0, tt * 512:(tt + 1) * 512], in0=sg[:], in1=pu[:], op=ALU.mult),
                                 [sg, pu], [AB])
                for dp in range(8):
                    wds, wdv = wload(wd, DFF, f0 * 128, nf, dp * 256, 256)
                    for dl in range(2):
                        d = dp * 2 + dl
                        for tt in range(2):
                            ps = k.psum()
                            for fc in range(nf):
                                k.op("pe", lambda e: e.matmul(ps[:], lhsT=wdv[:, fc, dl * 128:(dl + 1) * 128], rhs=ABv[:, fc, tt * 512:(tt + 1) * 512],
                                                               start=(fc == 0), stop=(fc == nf - 1)), [wds, AB], [ps], inc=(fc == nf - 1))
                            hs = Hv[:, d, tt * 512:(tt + 1) * 512]
                            k.op("dve", lambda e: e.scalar_tensor_tensor(out=hs, in0=ps[:], scalar=0.5, in1=hs, op0=ALU.mult, op1=ALU.add),
                                 [ps, H], [H])

        def proj(w_ap, rhs_buf, rhsv, evac, ntt=2, tw=512):
            for op_ in range(8):
                ws, wv = wload(w_ap, D, 0, 16, op_ * 256, 256)
                for ol in range(2):
                    oc = op_ * 2 + ol
                    for tt in range(ntt):
                        ps = k.psum()
                        for kc in range(16):
                            k.op("pe", lambda e: e.matmul(ps[:, 0:tw], lhsT=wv[:, kc, ol * 128:(ol + 1) * 128], rhs=rhsv[:, kc, tt * tw:(tt + 1) * tw],
                                                           start=(kc == 0), stop=(kc == 15)), [ws, rhs_buf], [ps], inc=(kc == 15))
                        evac(oc, tt, ps)

        o = [0]

        def al(nbytes, dt, pat=None, p1=128, **kw):
            v = av(o[0], nbytes, dt, pat, 0, p1, **kw)
            o[0] += nbytes
            return v

        P0 = k.view(None, "P0")
        CLin = al(5 * 2048, F32, "p (a n) -> p a n", a=5)
        k.dma("sp", CLin, cl_d.rearrange("a p n -> p a n"), P0, dIN)
        names = ["dt", "mag", "ang", "kf", "sn", "cs", "ar", "ai", "nr", "den", "fr", "fi", "t1", "t2", "pr", "pi", "bbr", "bbi", "t3", "t4"]
        cl = {n: al(2048, F32) for n in names}
        cli = al(2048, I32)
        WPc = al(32768, BF16, "p (q t r s) -> p q t r s", q=8, t=16, r=2)

        def abar(L, li_, lr_, ldt_, ki, eng="dve"):
            A = lambda fn: k.op("act", fn, [P0], [P0])
            V = lambda fn: k.op("dve", fn, [P0], [P0])
            A(lambda e: e.activation(out=L["dt"], in_=ldt_, func=AF.Exp))
            V(lambda e: e.tensor_tensor(out=L["mag"], in0=lr_, in1=L["dt"], op=ALU.mult))
            A(lambda e: e.activation(out=L["mag"], in_=L["mag"], func=AF.Exp))
            V(lambda e: e.tensor_tensor(out=L["ang"], in0=li_, in1=L["dt"], op=ALU.mult))
            V(lambda e: e.tensor_scalar(out=L["kf"], in0=L["ang"], scalar1=1.0 / TWO_PI, scalar2=None, op0=ALU.mult))
            V(lambda e: e.tensor_copy(out=ki, in_=L["kf"]))
            V(lambda e: e.tensor_copy(out=L["kf"], in_=ki))
            V(lambda e: e.scalar_tensor_tensor(out=L["ang"], in0=L["kf"], scalar=-TWO_PI, in1=L["ang"], op0=ALU.mult, op1=ALU.add))
            for nm, sh in (("sn", 0.0), ("cs", PI / 2)):
                V(lambda e: e.tensor_scalar(out=L["t1"], in0=L["ang"], scalar1=sh, scalar2=None, op0=ALU.add))
                V(lambda e: e.tensor_scalar(out=L["t2"], in0=L["t1"], scalar1=PI, scalar2=-TWO_PI, op0=ALU.is_gt, op1=ALU.mult))
                V(lambda e: e.tensor_tensor(out=L["t1"], in0=L["t1"], in1=L["t2"], op=ALU.add))
                V(lambda e: e.tensor_scalar(out=L["t2"], in0=L["t1"], scalar1=-PI, scalar2=TWO_PI, op0=ALU.is_lt, op1=ALU.mult))
                V(lambda e: e.tensor_tensor(out=L["t1"], in0=L["t1"], in1=L["t2"], op=ALU.add))
                V(lambda e: e.tensor_scalar(out=L["t1"], in0=L["t1"], scalar1=PI, scalar2=-PI, op0=ALU.min, op1=ALU.max))
                A(lambda e: e.activation(out=L[nm], in_=L["t1"], func=AF.Sin))
            V(lambda e: e.tensor_tensor(out=L["ar"], in0=L["mag"], in1=L["cs"], op=ALU.mult))
            V(lambda e: e.tensor_tensor(out=L["ai"], in0=L["mag"], in1=L["sn"], op=ALU.mult))
            V(lambda e: e.tensor_scalar(out=L["nr"], in0=L["ar"], scalar1=-1.0, scalar2=None, op0=ALU.add))
            V(lambda e: e.tensor_tensor(out=L["den"], in0=lr_, in1=lr_, op=ALU.mult))
            V(lambda e: e.tensor_tensor(out=L["t1"], in0=li_, in1=li_, op=ALU.mult))
            V(lambda e: e.tensor_tensor(out=L["den"], in0=L["den"], in1=L["t1"], op=ALU.add))
            V(lambda e: e.reciprocal(out=L["den"], in_=L["den"]))
            V(lambda e: e.tensor_tensor(out=L["t1"], in0=L["nr"], in1=lr_, op=ALU.mult))
            V(lambda e: e.tensor_tensor(out=L["t2"], in0=L["ai"], in1=li_, op=ALU.mult))
            V(lambda e: e.tensor_tensor(out=L["t1"], in0=L["t1"], in1=L["t2"], op=ALU.add))
            V(lambda e: e.tensor_tensor(out=L["fr"], in0=L["t1"], in1=L["den"], op=ALU.mult))
            V(lambda e: e.tensor_tensor(out=L["t1"], in0=L["ai"], in1=lr_, op=ALU.mult))
            V(lambda e: e.tensor_tensor(out=L["t2"], in0=L["nr"], in1=li_, op=ALU.mult))
            V(lambda e: e.tensor_tensor(out=L["t1"], in0=L["t1"], in1=L["t2"], op=ALU.subtract))
            V(lambda e: e.tensor_tensor(out=L["fi"], in0=L["t1"], in1=L["den"], op=ALU.mult))

        def cmul(L, outr, outi, ar_, ai_, br_, bi_, t1, t2):
            V = lambda fn: k.op("dve", fn, [P0], [P0])
            V(lambda e: e.tensor_tensor(out=t1, in0=ar_, in1=br_, op=ALU.mult))
            V(lambda e: e.tensor_tensor(out=t2, in0=ai_, in1=bi_, op=ALU.mult))
            V(lambda e: e.tensor_tensor(out=t1, in0=t1, in1=t2, op=ALU.subtract))
            V(lambda e: e.tensor_tensor(out=t2, in0=ar_, in1=bi_, op=ALU.mult))
            V(lambda e: e.tensor_tensor(out=outi, in0=ai_, in1=br_, op=ALU.mult))
            V(lambda e: e.tensor_tensor(out=outi, in0=outi, in1=t2, op=ALU.add))
            V(lambda e: e.tensor_copy(out=outr, in_=t1))

        V0 = lambda fn: k.op("dve", fn, [P0], [P0])
        ZT = al(16384, BF16)
        BZT = k.view(None, "ZT")
        k.op("dve", lambda e: e.memset(ZT, 0.0), [], [BZT])
        for q in range(8):
            for hh in range(2):
                k.dma("sp", WPd[q, :, hh * 8192:(hh + 1) * 8192], ZT, dWP, BZT)
        abar(cl, CLin[:, 1, :], CLin[:, 0, :], CLin[:, 2, :], cli)
        cmul(cl, cl["bbr"], cl["bbi"], cl["fr"], cl["fi"], CLin[:, 3, :], CLin[:, 4, :], cl["t1"], cl["t2"])
        V0(lambda e: e.memset(cl["pr"], 1.0))
        V0(lambda e: e.memset(cl["pi"], 0.0))
        for kk in range(16):
            tau = 15 - kk
            cmul(cl, cl["t3"], cl["t4"], cl["pr"], cl["pi"], cl["bbr"], cl["bbi"], cl["t1"], cl["t2"])
            V0(lambda e: e.tensor_copy(out=WPc[:, :, tau, 0, :], in_=cl["t3"].rearrange("p (q s) -> p q s", q=8)))
            V0(lambda e: e.tensor_copy(out=WPc[:, :, tau, 1, :], in_=cl["t4"].rearrange("p (q s) -> p q s", q=8)))
            if kk < 15:
                cmul(cl, cl["pr"], cl["pi"], cl["pr"], cl["pi"], cl["ar"], cl["ai"], cl["t1"], cl["t2"])
        k.barrier()
        WPdv = WPd.rearrange("q c (g x) -> q c g x", g=8)
        for gi in range(8):
            for q in range(8):
                k.dma("sp", WPdv[q, 16 * gi:16 * gi + 16, gi, :], WPc[16 * gi:16 * gi + 16, q].rearrange("p t r s -> p (t r s)"), dWP, P0)

        k.barrier()
        o[0] = 0
        SL3 = al(3 * 256, F32, "p (a n) -> p a n", a=3)
        k.dma("sp", SL3[0:64], sl3_d.rearrange("a p n -> p a n"), P0, dIN)
        SL4 = al(4 * 4096, F32, "p (a n) -> p a n", a=4)
        k.dma("sp", SL4[0:64], sl4_d.rearrange("a p n -> p a n"), P0, dIN)
        sl = {n: al(256, F32)[0:64] for n in names}
        sli = al(256, I32)[0:64]
        abar(sl, SL3[0:64, 1, :], SL3[0:64, 0, :], SL3[0:64, 2, :], sli)
        PWr = al(17 * 256, F32, "p (k g) -> p k g", k=17)[0:64]
        PWi = al(17 * 256, F32, "p (k g) -> p k g", k=17)[0:64]
        V0(lambda e: e.memset(PWr[:, 0, :], 1.0))
        V0(lambda e: e.memset(PWi[:, 0, :], 0.0))
        for kk in range(16):
            cmul(sl, PWr[:, kk + 1, :], PWi[:, kk + 1, :], PWr[:, kk, :], PWi[:, kk, :], sl["ar"], sl["ai"], sl["t1"], sl["t2"])
        AT = k.sb("AT", [64, 2, 64], F32)
        k.op("dve", lambda e: e.tensor_copy(out=AT[:, 0, :], in_=PWr[:, 16, :]), [P0], [AT])
        k.op("dve", lambda e: e.tensor_copy(out=AT[:, 1, :], in_=PWi[:, 16, :]), [P0], [AT])
        Cre = SL4[0:64, 0, :].rearrange("p (g h) -> p g h", h=16)
        Cim = SL4[0:64, 1, :].rearrange("p (g h) -> p g h", h=16)
        Bres = SL4[0:64, 2, :].rearrange("p (g h) -> p g h", h=16)
        Bims = SL4[0:64, 3, :].rearrange("p (g h) -> p g h", h=16)
        frb = sl["fr"].unsqueeze(2).broadcast_to([64, 64, 16])
        fib = sl["fi"].unsqueeze(2).broadcast_to([64, 64, 16])
        BBr = al(4096, F32, "p (g h) -> p g h", h=16)[0:64]
        BBi = al(4096, F32, "p (g h) -> p g h", h=16)[0:64]
        TA = al(4096, F32, "p (g h) -> p g h", h=16)[0:64]
        TB = al(4096, F32, "p (g h) -> p g h", h=16)[0:64]
        V0(lambda e: e.tensor_tensor(out=TA, in0=Bres, in1=frb, op=ALU.mult))
        V0(lambda e: e.tensor_tensor(out=TB, in0=Bims, in1=fib, op=ALU.mult))
        V0(lambda e: e.tensor_tensor(out=BBr, in0=TA, in1=TB, op=ALU.subtract))
        V0(lambda e: e.tensor_tensor(out=TA, in0=Bims, in1=frb, op=ALU.mult))
        V0(lambda e: e.tensor_tensor(out=TB, in0=Bres, in1=fib, op=ALU.mult))
        V0(lambda e: e.tensor_tensor(out=TA, in0=TA, in1=TB, op=ALU.add))
        V0(lambda e: e.tensor_scalar(out=BBi, in0=TA, scalar1=-1.0, scalar2=None, op0=ALU.mult))
        Dr = al(8 * 17 * 16 * 4, F32, "p (g k h) -> p g k h", g=8, k=17)[0:64]
        Di = al(8 * 17 * 16 * 4, F32, "p (g k h) -> p g k h", g=8, k=17)[0:64]
        T1 = al(8 * 17 * 16 * 4, F32, "p (g k h) -> p g k h", g=8, k=17)[0:64]
        T2 = al(8 * 17 * 16 * 4, F32, "p (g k h) -> p g k h", g=8, k=17)[0:64]
        BPr = al(8 * 128 * 4, F32, "p (g n) -> p g n", g=8)[0:64]
        BPi = al(8 * 128 * 4, F32, "p (g n) -> p g n", g=8)[0:64]
        V0(lambda e: e.memset(BPr, 0.0))
        V0(lambda e: e.memset(BPi, 0.0))
        WCts = [al(16384, BF16, "p (g t r n) -> p g t r n", g=8, t=4, r=2)[0:64] for _ in range(2)]
        BWCT = [k.view(None, "WCT0"), k.view(None, "WCT1")]
        for i_ in range(2):
            k.op("dve", lambda e: e.memset(WCts[i_], 0.0), [], [BWCT[i_]])
        wci = [0]
        Kc = al(1024, F32, "p (k h) -> p k h", k=16)
        KB = al(4096, BF16, "p (k g h) -> p k g h", k=16, g=8)
        assert o[0] <= 131072, o[0]
        for q in range(8):
            gs = slice(8 * q, 8 * q + 8)
            pwr = PWr[:, :, gs].rearrange("p k g -> p g k").unsqueeze(3).broadcast_to([64, 8, 17, 16])
            pwi = PWi[:, :, gs].rearrange("p k g -> p g k").unsqueeze(3).broadcast_to([64, 8, 17, 16])
            cre = Cre[:, gs, :].unsqueeze(2).broadcast_to([64, 8, 17, 16])
            cim = Cim[:, gs, :].unsqueeze(2).broadcast_to([64, 8, 17, 16])
            V0(lambda e: e.tensor_tensor(out=T1, in0=cre, in1=pwr, op=ALU.mult))
            V0(lambda e: e.tensor_tensor(out=T2, in0=cim, in1=pwi, op=ALU.mult))
            V0(lambda e: e.tensor_tensor(out=Dr, in0=T1, in1=T2, op=ALU.subtract))
            V0(lambda e: e.tensor_tensor(out=T1, in0=cre, in1=pwi, op=ALU.mult))
            V0(lambda e: e.tensor_tensor(out=T2, in0=cim, in1=pwr, op=ALU.mult))
            V0(lambda e: e.tensor_tensor(out=Di, in0=T1, in1=T2, op=ALU.add))
            for gi in range(8):
                V0(lambda e: e.tensor_copy(out=BPr[:, gi, 16 * gi:16 * gi + 16], in_=BBr[:, 8 * q + gi, :]))
                V0(lambda e: e.tensor_copy(out=BPi[:, gi, 16 * gi:16 * gi + 16], in_=BBi[:, 8 * q + gi, :]))
            ps = k.psum()
            n = 0
            for gi in range(8):
                for (bp, dd) in ((BPr, Dr), (BPi, Di)):
                    k.op("pe", lambda e: e.matmul(ps[:, 0:256], lhsT=bp[:, gi, :], rhs=dd[:, gi, 0:16, :], start=(n == 0), stop=(n == 15)),
                         [P0], [ps], inc=(n == 15))
                    n += 1
            k.op("act", lambda e: e.activation(out=Kc, in_=ps[:, 0:256].rearrange("p (k h) -> p k h", k=16), func=AF.Copy), [ps], [P0])
            for kk in range(16):
                V0(lambda e: e.tensor_tensor(out=KB[:, kk], in0=Kc[:, kk].unsqueeze(1).broadcast_to([128, 8, 16]),
                                             in1=mask8[:, :].unsqueeze(2).broadcast_to([128, 8, 16]), op=ALU.mult))
            k.dma("sp", KBd[q], KB.rearrange("p k g h -> p (k g h)"), dKB, P0)
            WCdq = WCd[q].rearrange("p (g t r n) -> p g t r n", g=8, t=16, r=2)
            for tq in range(4):
                wt = WCts[wci[0] % 2]; bw = BWCT[wci[0] % 2]; wci[0] += 1
                for gi in range(8):
                    k.op("dve", lambda e: e.tensor_copy(out=wt[:, gi, :, 0, 16 * gi:16 * gi + 16], in_=Dr[:, gi, 1 + 4 * tq:5 + 4 * tq, :]), [P0], [bw])
                    k.op("dve", lambda e: e.tensor_scalar(out=wt[:, gi, :, 1, 16 * gi:16 * gi + 16], in0=Di[:, gi, 1 + 4 * tq:5 + 4 * tq, :], scalar1=-1.0, scalar2=None, op0=ALU.mult),
                         [P0], [bw])
                k.dma("sp", WCdq[:, :, 4 * tq:4 * tq + 4], wt, dWC, bw)
        k.barrier()

        xv = xT.rearrange("(c p) t -> p c t", p=128)
        H1v = H1d.rearrange("(c p) t -> p c t", p=128)
        UPv = UPd.rearrange("(c p) t -> p c t", p=128)
        YLv = YLd.rearrange("(c p) t -> p c t", p=128)
        MRGv = MRGd.rearrange("(c p) t -> p c t", p=128)
        PSTv = PSTd.rearrange("p (g r c) -> p g r c", g=64, r=2)
        for s in range(2):
            t0 = s * ST
            for c4 in range(4):
                k.dma("sp", Hv[:, 4 * c4:4 * c4 + 4, :], xv[:, 4 * c4:4 * c4 + 4, t0:t0 + ST], H, dIN)
            rmsnorm(H, lambda kc, tt: Hv[:, kc, tt * 512:(tt + 1) * 512], XN, lambda kc, tt: XNv[:, kc, tt * 512:(tt + 1) * 512], 16, GC["ffn1"], 2, 2048.0)
            ffn(w1g, w1u, w1d)
            k.dma("sp", H1v[:, :, t0:t0 + ST], Hv, dH1, H)
            rmsnorm(H, lambda kc, tt: Hv[:, kc, tt * 512:(tt + 1) * 512], XN, lambda kc, tt: XNv[:, kc, tt * 512:(tt + 1) * 512], 16, GC["mix"], 2, 2048.0)

            def ev_in(oc, tt, ps):
                if oc < 8:
                    k.op("act", lambda e: e.activation(out=ABv[:, oc, tt * 512:(tt + 1) * 512], in_=ps[:], func=AF.Copy), [ps], [AB])
                else:
                    sb_ = rot(stb, "stb")
                    k.op("act", lambda e: e.activation(out=sb_[:], in_=ps[:], func=AF.Copy), [ps], [sb_])
                    k.dma("sp", UPv[:, oc - 8, t0 + tt * 512:t0 + (tt + 1) * 512], sb_[:], dUP, sb_)
            proj(w_in, XN, XNv, ev_in)
            for q in range(8):
                kbs = k.wslot()
                kbv = kbs[:, 0:2048].rearrange("p (k n) -> p k n", k=16)
                k.dma("pool", kbs[:, 0:2048], KBd[q], kbs, dKB)
                for tt in range(2):
                    ps = k.psum()
                    uu = ABv[:, q, tt * 512:(tt + 1) * 512].rearrange("p (c j) -> p c j", j=16)
                    pp = ps[:, :].rearrange("p (c j) -> p c j", j=16)
                    for kk in range(16):
                        k.op("pe", lambda e: e.matmul(pp[:, :, kk:16], lhsT=kbv[:, kk, :], rhs=uu[:, :, 0:16 - kk], start=(kk == 0), stop=(kk == 15)),
                             [kbs, AB], [ps], inc=(kk == 15))
                    yl = rot(stg, "stg")
                    k.op("dve", lambda e: e.scalar_tensor_tensor(out=yl[:], in0=ABv[:, q, tt * 512:(tt + 1) * 512], scalar=GV[:, GC["dskip"] + q:GC["dskip"] + q + 1],
                                                                  in1=ps[:], op0=ALU.mult, op1=ALU.add), [AB, ps, GV], [yl])
                    k.dma("sp", YLv[:, q, t0 + tt * 512:t0 + (tt + 1) * 512], yl[:], dYL, yl)
                uq = ABv[:, q, :].rearrange("p (c j) -> p c j", j=16)
                for gp in range(4):
                    wps = k.wslot()
                    wpv = wps[:, :].rearrange("p (g t r s) -> p g t r s", g=2, t=16, r=2)
                    k.dma("pool", wps[:, :], WPd[q, :, gp * 4096:(gp + 1) * 4096], wps, dWP)
                    ps = k.psum()
                    for gl in range(2):
                        for tau in range(16):
                            o_ = (gl * 16 + tau) * 128
                            k.op("pe", lambda e: e.matmul(ps[:, gl * 64:(gl + 1) * 64], lhsT=wps[:, o_:o_ + 128], rhs=uq[:, :, tau],
                                                           start=(tau == 0), stop=(tau == 15)), [wps, AB], [ps], inc=(tau == 15))
                    pst = rot(stg, "stg")
                    k.op("act", lambda e: e.activation(out=pst[:, 0:128], in_=ps[:, 0:128], func=AF.Copy), [ps], [pst])
                    g0 = 8 * q + 2 * gp
                    for ri in range(2):
                        k.dma("sp", PSTv[:, g0:g0 + 2, ri, s * 64:(s + 1) * 64], pst[64 * ri:64 * ri + 64, 0:128].rearrange("p (g c) -> p g c", g=2), dPST, pst)
        k.barrier()

        o[0] = 0
        SSb = al(64 * 2 * 129 * 2 + 0, BF16, "p (g r c) -> p g r c", g=64, r=2)[0:64]
        SS = al(64 * 2 * 129 * 4, F32, "p (g r c) -> p g r c", g=64, r=2)[0:64]
        PX = k.view(None, "PX")
        XS = k.sb("XS", [128, 256], F32)
        XR = k.sb("XR", [128, 256], F32)
        sct = [k.sb("sc%d" % i, [64, 64, 2], F32) for i in range(2)]
        A1 = k.sb("A1", [64, 64, 2], F32)
        A2 = k.sb("A2", [64, 64, 2], F32)
        k.op("dve", lambda e: e.tensor_copy(out=A1[:, :, 0], in_=AT[:, 0, :]), [AT], [A1])
        k.op("dve", lambda e: e.tensor_copy(out=A1[:, :, 1], in_=AT[:, 0, :]), [AT], [A1])
        k.op("dve", lambda e: e.tensor_scalar(out=A2[:, :, 0], in0=AT[:, 1, :], scalar1=-1.0, scalar2=None, op0=ALU.mult), [AT], [A2])
        k.op("dve", lambda e: e.tensor_copy(out=A2[:, :, 1], in_=AT[:, 1, :]), [AT], [A2])

        def scan():
            VX = lambda fn: k.op("dve", fn, [PX, A1, A2], [PX])
            for c in range(128):
                prev = SS[:, :, :, c]
                cur = SS[:, :, :, c + 1]
                t1, t2 = sct[0], sct[1]
                VX(lambda e: e.tensor_tensor(out=t1[:], in0=prev, in1=A1[:], op=ALU.mult))
                VX(lambda e: e.tensor_tensor(out=t2[:, :, 0], in0=SS[:, :, 1, c], in1=A2[:, :, 0], op=ALU.mult))
                VX(lambda e: e.tensor_tensor(out=t2[:, :, 1], in0=SS[:, :, 0, c], in1=A2[:, :, 1], op=ALU.mult))
                VX(lambda e: e.tensor_tensor(out=t1[:], in0=t1[:], in1=t2[:], op=ALU.add))
                VX(lambda e: e.tensor_tensor(out=cur, in0=cur, in1=t1[:], op=ALU.add))

        k.dma("sp", SS[:, :, :, 1:129], PSTv, PX, dPST)
        k.op("dve", lambda e: e.memset(SS[:, :, :, 0], 0.0), [PX], [PX])
        scan()
        k.op("dve", lambda e: e.memset(XS[:], 0.0), [], [XS])
        k.op("dve", lambda e: e.tensor_copy(out=XS[0:64, 0:128].rearrange("p (g r) -> p g r", r=2), in_=SS[:, :, :, 128]), [PX], [XS])
        hb = rot(stb, "stb")
        k.dma("sp", hb[:, 0:128].rearrange("p (q j) -> p q j", q=8), UPv[:, :, NT - 16:NT], hb, dUP)
        k.op("dve", lambda e: e.tensor_copy(out=XS[:, 128:256], in_=hb[:, 0:128]), [hb], [XS])
        k.dma("sp", XINd.ap(), XS[:], dXIN, XS)
        ccsem = es.enter_context(nc.semaphore("ccsem"))
        k._deps("pool", [dXIN], [dXOUT])
        nc.gpsimd.collective_compute("AllGather", ALU.bypass, replica_groups=[[0, 1], [2, 3], [4, 5], [6, 7]],
                                     ins=[XINd.ap().opt()], outs=[XOUTd.ap().opt()]).then_inc(ccsem)
        k.prog["pool"].append(("i", id(ccsem), 1))
        dXOUT.lw = (("d", ccsem), 1)
        dXOUT.rd = {}
        dXIN.rd[("d", ccsem)] = 1
        k.dma("sp", XR[:], XOUTd.ap()[0:128, :], XR, dXOUT)
        k.op("dve", lambda e: e.tensor_scalar(out=XR[:], in0=XR[:], scalar1=flag[:, 0:1], scalar2=None, op0=ALU.mult), [XR, flag], [XR])
        k.dma("sp", SS[:, :, :, 1:129], PSTv, PX, dPST)
        k.op("dve", lambda e: e.tensor_copy(out=SS[:, :, :, 0], in_=XR[0:64, 0:128].rearrange("p (g r) -> p g r", r=2)), [XR, PX], [PX])
        scan()
        k.op("dve", lambda e: e.tensor_copy(out=SSb[:, 0:32], in_=SS[:, 0:32]), [PX], [PX])
        k.op("act", lambda e: e.activation(out=SSb[:, 32:64], in_=SS[:, 32:64], func=AF.Copy), [PX], [PX])

        k.barrier()
        o[0] = 33024
        YGb = al(32768, BF16, "p (q t) -> p q t", q=8)
        WGL = al(16384, BF16, "p (c n) -> p c n", c=8)
        YLt = al(8192, F32)
        Yt = al(8192, F32)
        Tt = al(8192, F32)
        assert o[0] <= 131072, o[0]
        BYG = k.view(None, "YGb"); BWGL = k.view(None, "WGL"); BYL = k.view(None, "YLt"); BY = k.view(None, "Yt"); BT = k.view(None, "Tt")
        k.dma("pool", WGL, w_glu.rearrange("(c p) n -> p c n", p=128), BWGL, dIN)
        for q in range(8):
            k.dma("pool", YLt, YLv[:, q, :], BYL, dYL)
            pss = []
            for tq in range(4):
                ps = k.psum()
                wl = []
                for gh in range(2):
                    wcs = k.wslot()
                    wcv = wcs[0:64, :].rearrange("p (g t r n) -> p g t r n", g=4, t=4, r=2)
                    src = WCd[q].rearrange("p (g t r n) -> p g t r n", g=8, t=16, r=2)[:, 4 * gh:4 * gh + 4, 4 * tq:4 * tq + 4]
                    k.dma("pool", wcv, src, wcs, dWC)
                    wl.append((wcs, wcv))
                for tl in range(4):
                    n = 0
                    for gh in range(2):
                        wcs, wcv = wl[gh]
                        for gl in range(4):
                            gi = 4 * gh + gl
                            for ri in range(2):
                                k.op("pe", lambda e: e.matmul(ps[:, tl * 128:(tl + 1) * 128], lhsT=wcv[:, gl, tl, ri, :], rhs=SSb[:, 8 * q + gi, ri, 0:128],
                                                               start=(n == 0), stop=(n == 15)), [wcs, PX], [ps], inc=(n == 15))
                                n += 1
                pss.append(ps)
            for tq in range(4):
                ps = pss[tq]
                yv = Yt.rearrange("p (c j) -> p c j", j=16)[:, :, 4 * tq:4 * tq + 4]
                lv = YLt.rearrange("p (c j) -> p c j", j=16)[:, :, 4 * tq:4 * tq + 4]
                pv = ps[:, :].rearrange("p (j c) -> p c j", j=4)
                k.op("dve", lambda e: e.tensor_tensor(out=yv, in0=lv, in1=pv, op=ALU.add), [BYL, ps], [BY])
            k.op("act", lambda e: e.activation(out=Tt, in_=Yt, func=AF.Square), [BY], [BT])
            k.op("dve", lambda e: e.tensor_scalar(out=Tt, in0=Tt, scalar1=0.044715, scalar2=1.0, op0=ALU.mult, op1=ALU.add), [BT], [BT])
            k.op("dve", lambda e: e.tensor_tensor(out=Tt, in0=Tt, in1=Yt, op=ALU.mult), [BT, BY], [BT])
            k.op("act", lambda e: e.activation(out=Tt, in_=Tt, func=AF.Tanh, scale=0.7978845608028654), [BT], [BT])
            k.op("dve", lambda e: e.tensor_scalar(out=Tt, in0=Tt, scalar1=1.0, scalar2=0.5, op0=ALU.add, op1=ALU.mult), [BT], [BT])
            k.op("dve", lambda e: e.tensor_tensor(out=YGb[:, q, :], in0=Tt, in1=Yt, op=ALU.mult), [BT, BY], [BYG])
        o1 = o[0]
        YM = al(16384, F32, "p (c t) -> p c t", c=8)
        BYM = k.view(None, "YM")
        for tt in range(4):
            for oc in range(8):
                ps = k.psum()
                for kc in range(8):
                    k.op("pe", lambda e: e.matmul(ps[:], lhsT=WGL[:, kc, oc * 128:(oc + 1) * 128], rhs=YGb[:, kc, tt * 512:(tt + 1) * 512],
                                                   start=(kc == 0), stop=(kc == 7)), [BWGL, BYG], [ps], inc=(kc == 7))
                sg = rot(stg, "stg")
                k.op("act", lambda e: e.activation(out=sg[:], in_=ps[:], func=AF.Sigmoid, bias=GV[:, GC["bglu"] + oc:GC["bglu"] + oc + 1]), [ps, GV], [sg])
                k.op("dve", lambda e: e.tensor_tensor(out=YM[:, oc, :], in0=YGb[:, oc, tt * 512:(tt + 1) * 512], in1=sg[:], op=ALU.mult), [BYG, sg], [BYM])
            mt = {}

            def dstf(kc, t_):
                b = rot(stb, "stb")
                mt[kc] = b
                return b[:]
            ps = k.psum()
            for kc in range(8):
                sq = rot(sqt, "sq")
                k.op("act", lambda e: e.activation(out=sq[:], in_=YM[:, kc, :], func=AF.Square), [BYM], [sq])
                k.op("pe", lambda e: e.matmul(ps[:], lhsT=ones[:], rhs=sq[:], start=(kc == 0), stop=(kc == 7)), [ones, sq], [ps], inc=True)
            rs = rot(rst, "rs")
            k.op("act", lambda e: e.activation(out=rs[:], in_=ps[:], func=AF.Sqrt, bias=EPS[:, 0:1], scale=1.0 / 1024.0), [ps, EPS], [rs])
            k.op("dve", lambda e: e.reciprocal(out=rs[:], in_=rs[:]), [rs], [rs])
            for kc in range(8):
                b = rot(stb, "stb")
                k.op("dve", lambda e: e.scalar_tensor_tensor(out=b[:], in0=YM[:, kc, :], scalar=GV[:, GC["ossm"] + kc:GC["ossm"] + kc + 1], in1=rs[:],
                                                              op0=ALU.mult, op1=ALU.mult), [BYM, rs, GV], [b])
                k.dma("sp", MRGv[:, kc, tt * 512:(tt + 1) * 512], b[:], dMRG, b)
        k.barrier()
        o[0] = 0
        PL = al(32768, BF16, "p (c t) -> p c t", c=8)
        Vt = al(2064 * 4, F32)
        Sa = al(2064 * 4, F32)
        Sb_ = al(2064 * 4, F32)
        Ub = al(4096, BF16)
        ZT2 = al(16384, F32, "p (c t) -> p c t", c=8)
        assert o[0] <= 131072
        BPL = k.view(None, "PL"); BV = k.view(None, "Vt"); BSa = k.view(None, "Sa"); BSb = k.view(None, "Sb"); BU = k.view(None, "Ub"); BZ = k.view(None, "ZT2")
        WIN = (2, 4, 8, 16)
        XRh = XR[:, 128:256].rearrange("p (q j) -> p q j", q=8)
        for pc in range(8):
            gi = pc // 2
            w = WIN[gi]
            k.dma("pool", Ub, UPv[:, pc, :], BU, dUP)
            k.op("act", lambda e: e.activation(out=Vt[:, 16:2064], in_=Ub, func=AF.Copy), [BU], [BV])
            k.op("dve", lambda e: e.tensor_copy(out=Vt[:, 0:16], in_=XRh[:, pc, :]), [XR], [BV])
            cur, curb = Vt, BV
            sh = 1
            bufs2 = [(Sa, BSa), (Sb_, BSb)]
            bi = 0
            while sh < w:
                nxt, nxtb = bufs2[bi]
                bi ^= 1
                k.op("dve", lambda e: e.tensor_tensor(out=nxt[:, sh:2064], in0=cur[:, sh:2064], in1=cur[:, 0:2064 - sh], op=ALU.add), [curb], [nxtb])
                cur, curb = nxt, nxtb
                sh *= 2
            k.op("dve", lambda e: e.scalar_tensor_tensor(out=PL[:, pc, :], in0=cur[:, 16:2064], scalar=1.0 / w, in1=Vt[:, 16:2064], op0=ALU.mult, op1=ALU.subtract),
                 [curb, BV], [BPL])
            nxt, nxtb = bufs2[bi]
            k.op("dve", lambda e: e.tensor_tensor(out=nxt[:, 0:16], in0=cur[:, 16:32], in1=icnt[:, 16 * gi:16 * gi + 16], op=ALU.mult), [curb, icnt], [nxtb])
            k.op("dve", lambda e: e.tensor_tensor(out=PL[:, pc, 0:16], in0=nxt[:, 0:16], in1=Vt[:, 16:32], op=ALU.subtract), [nxtb, BV], [BPL])
        wpl = []
        for gi in range(4):
            ws = k.wslot()
            v = ws[:, 0:512].rearrange("p (c n) -> p c n", c=2)
            k.dma("pool", v, w_pool[gi].rearrange("(c p) n -> p c n", p=128), ws, dIN)
            wpl.append((ws, v))
        for tt in range(4):
            for gi in range(4):
                ws, v = wpl[gi]
                for dc in range(2):
                    ps = k.psum()
                    for cc in range(2):
                        k.op("pe", lambda e: e.matmul(ps[:], lhsT=v[:, cc, dc * 128:(dc + 1) * 128], rhs=PL[:, 2 * gi + cc, tt * 512:(tt + 1) * 512],
                                                       start=(cc == 0), stop=(cc == 1)), [ws, BPL], [ps], inc=(cc == 1))
                    oc = 2 * gi + dc
                    k.op("act", lambda e: e.activation(out=ZT2[:, oc, :], in_=ps[:], func=AF.Copy, scale=GV[:, GC["pscale"] + oc:GC["pscale"] + oc + 1]), [ps, GV], [BZ])
            ps = k.psum()
            for kc in range(8):
                sq = rot(sqt, "sq")
                k.op("act", lambda e: e.activation(out=sq[:], in_=ZT2[:, kc, :], func=AF.Square), [BZ], [sq])
                k.op("pe", lambda e: e.matmul(ps[:], lhsT=ones[:], rhs=sq[:], start=(kc == 0), stop=(kc == 7)), [ones, sq], [ps], inc=True)
            rs = rot(rst, "rs")
            k.op("act", lambda e: e.activation(out=rs[:], in_=ps[:], func=AF.Sqrt, bias=EPS[:, 0:1], scale=1.0 / 1024.0), [ps, EPS], [rs])
            k.op("dve", lambda e: e.reciprocal(out=rs[:], in_=rs[:]), [rs], [rs])
            for kc in range(8):
                b = rot(stb, "stb")
                k.op("dve", lambda e: e.scalar_tensor_tensor(out=b[:], in0=ZT2[:, kc, :], scalar=GV[:, GC["opool"] + kc:GC["opool"] + kc + 1], in1=rs[:],
                                                              op0=ALU.mult, op1=ALU.mult), [BZ, rs, GV], [b])
                k.dma("sp", MRGv[:, 8 + kc, tt * 512:(tt + 1) * 512], b[:], dMRG, b)
        k.barrier()

        MN = k.view(av(98304, 8192, BF16, "p (c m) -> p c m", c=16), "MN")
        KT = k.sb("KT", [128, 16, 256], BF16)
        VM = k.sb("VM", [128, 2, 2048], BF16)
        MF = Hv[:, :, 0:256]
        k.dma("sp", MF, memT.rearrange("(c p) m -> p c m", p=128), H, dIN)
        rmsnorm(H, lambda kc, tt: Hv[:, kc, 0:256], MN, lambda kc, tt: MN[:, kc, :], 16, GC["mem"], 1, 2048.0, tw=256)

        def ev_k(oc, tt, ps):
            k.op("act", lambda e: e.activation(out=KT[:, oc, :], in_=ps[:, 0:256], func=AF.Copy), [ps], [KT])
        proj(w_k, MN, MN, ev_k, ntt=1, tw=256)
        for op_ in range(8):
            ws, wv = wload(w_v, D, 0, 16, op_ * 256, 256)
            for mc in range(2):
                ps = k.psum()
                for kc in range(16):
                    k.op("pe", lambda e: e.matmul(ps[:, 0:256], lhsT=MN[:, kc, mc * 128:(mc + 1) * 128], rhs=wv[:, kc, :], start=(kc == 0), stop=(kc == 15)),
                         [ws, MN], [ps], inc=(kc == 15))
                k.op("act", lambda e: e.activation(out=VM[:, mc, op_ * 256:(op_ + 1) * 256], in_=ps[:, 0:256], func=AF.Copy), [ps], [VM])

        yv_ = yT.rearrange("(c p) t -> p c t", p=128)
        Et = [k.sb("Et%d" % i, [128, 2, 512], BF16) for i in range(2)]
        ecnt = [0]
        for s in range(2):
            t0 = s * ST
            k.dma("pool", Hv, H1v[:, :, t0:t0 + ST], H, dH1)
            k.dma("pool", XNv, MRGv[:, :, t0:t0 + ST], XN, dMRG)

            def ev_add(oc, tt, ps):
                hs = Hv[:, oc, tt * 512:(tt + 1) * 512]
                k.op("dve", lambda e: e.tensor_tensor(out=hs, in0=hs, in1=ps[:], op=ALU.add), [H, ps], [H])
            proj(w_out, XN, XNv, ev_add)
            rmsnorm(H, lambda kc, tt: Hv[:, kc, tt * 512:(tt + 1) * 512], XN, lambda kc, tt: XNv[:, kc, tt * 512:(tt + 1) * 512], 16, GC["xattn"], 2, 2048.0)

            def ev_q(oc, tt, ps):
                k.op("act", lambda e: e.activation(out=ABv[:, oc, tt * 512:(tt + 1) * 512], in_=ps[:], func=AF.Copy, scale=512.0 ** -0.5), [ps], [AB])
            proj(w_q, XN, XNv, ev_q)
            for hh in range(4):
                for tt in range(2):
                    et = Et[ecnt[0] % 2]; ecnt[0] += 1
                    for mc in range(2):
                        ps = k.psum()
                        for dc in range(4):
                            k.op("pe", lambda e: e.matmul(ps[:], lhsT=KT[:, hh * 4 + dc, mc * 128:(mc + 1) * 128], rhs=ABv[:, hh * 4 + dc, tt * 512:(tt + 1) * 512],
                                                           start=(dc == 0), stop=(dc == 3)), [KT, AB], [ps], inc=(dc == 3))
                        k.op("act", lambda e: e.activation(out=et[:, mc, :], in_=ps[:], func=AF.Exp), [ps], [et])
                    ps2 = k.psum()
                    for mc in range(2):
                        k.op("pe", lambda e: e.matmul(ps2[:], lhsT=ones[:], rhs=et[:, mc, :], start=(mc == 0), stop=(mc == 1)), [ones, et], [ps2], inc=(mc == 1))
                    rs = rot(rst, "rs")
                    k.op("dve", lambda e: e.reciprocal(out=rs[:], in_=ps2[:]), [ps2], [rs])
                    for dvc in range(4):
                        ps3 = k.psum()
                        for mc in range(2):
                            k.op("pe", lambda e: e.matmul(ps3[:], lhsT=VM[:, mc, hh * 512 + dvc * 128:hh * 512 + (dvc + 1) * 128], rhs=et[:, mc, :],
                                                           start=(mc == 0), stop=(mc == 1)), [VM, et], [ps3], inc=(mc == 1))
                        k.op("dve", lambda e: e.tensor_tensor(out=XNv[:, hh * 4 + dvc, tt * 512:(tt + 1) * 512], in0=ps3[:], in1=rs[:], op=ALU.mult), [ps3, rs], [XN])
            proj(w_o, XN, XNv, ev_add)
            rmsnorm(H, lambda kc, tt: Hv[:, kc, tt * 512:(tt + 1) * 512], XN, lambda kc, tt: XNv[:, kc, tt * 512:(tt + 1) * 512], 16, GC["ffn2"], 2, 2048.0)
            ffn(w2g, w2u, w2d)
            for tt in range(2):
                ps = k.psum()
                for kc in range(16):
                    sq = rot(sqt, "sq")
                    k.op("act", lambda e: e.activation(out=sq[:], in_=Hv[:, kc, tt * 512:(tt + 1) * 512], func=AF.Square), [H], [sq])
                    k.op("pe", lambda e: e.matmul(ps[:], lhsT=ones[:], rhs=sq[:], start=(kc == 0), stop=(kc == 15)), [ones, sq], [ps], inc=True)
                rs = rot(rst, "rs")
                k.op("act", lambda e: e.activation(out=rs[:], in_=ps[:], func=AF.Sqrt, bias=EPS[:, 0:1], scale=1.0 / 2048.0), [ps, EPS], [rs])
                k.op("dve", lambda e: e.reciprocal(out=rs[:], in_=rs[:]), [rs], [rs])
                for kc in range(16):
                    b = rot(stg, "stg")
                    k.op("dve", lambda e: e.scalar_tensor_tensor(out=b[:], in0=Hv[:, kc, tt * 512:(tt + 1) * 512], scalar=GV[:, GC["final"] + kc:GC["final"] + kc + 1],
                                                                  in1=rs[:], op0=ALU.mult, op1=ALU.mult), [H, rs, GV], [b])
                    k.dma("sp", yv_[:, kc, t0 + tt * 512:t0 + (tt + 1) * 512], b[:], dY, b)
        k.finish([dY, dDBG])
        build_nc.sim = k.simulate()
    return nc


def _vec(v):
    v = np.asarray(v, np.float32).reshape(-1, 128)
    return np.ascontiguousarray(v.T)


def prep_inputs(inp):
    L = 0
    f = lambda a: np.ascontiguousarray(np.asarray(a, np.float32))
    x = f(inp["x"]); mem = f(inp["mem"])
    gv = np.concatenate([
        _vec(inp["g_ffn1"][L]), _vec(inp["g_mix"][L]), _vec(inp["g_xattn"][L]), _vec(inp["g_mem"][L]),
        _vec(inp["g_ffn2"][L]), _vec(inp["g_final"]), _vec(inp["g_out_ssm"][L]), _vec(inp["g_out_pool"][L]),
        _vec(inp["ssm_d"][L]), _vec(inp["b_glu"][L]), _vec(inp["pool_scale"][L])], axis=1)
    gv = np.ascontiguousarray(gv, np.float32)
    assert gv.shape == (128, 136)
    a_re = f(inp["ssm_a_re"][L]); a_im = f(inp["ssm_a_im"][L]); ldt = f(inp["ssm_log_dt"][L])
    b_re = f(inp["ssm_b_re"][L]); b_im = f(inp["ssm_b_im"][L])
    c_re = f(inp["ssm_c_re"][L]); c_im = f(inp["ssm_c_im"][L])

    def cl_gp(a):
        a4 = a.reshape(8, 8, 64)
        r = np.broadcast_to(a4[:, :, None, :], (8, 8, 16, 64))
        return np.ascontiguousarray(r.transpose(1, 2, 0, 3).reshape(128, 512))

    def cl_b(b):
        b5 = b.reshape(8, 8, 64, 16)
        return np.ascontiguousarray(b5.transpose(1, 3, 0, 2).reshape(128, 512))
    ldt_gp = np.broadcast_to(ldt[:, None], (64, 64))
    ssm_cl = np.stack([cl_gp(a_re), cl_gp(a_im), cl_gp(ldt_gp), cl_b(b_re), cl_b(b_im)]).astype(np.float32)
    ssm_sl3 = np.stack([a_re.T, a_im.T, ldt_gp.T]).astype(np.float32)
    ssm_sl3 = np.ascontiguousarray(ssm_sl3)
    sl_c = lambda c_: np.ascontiguousarray(c_.transpose(2, 0, 1).reshape(64, 1024))
    sl_b = lambda b_: np.ascontiguousarray(b_.transpose(1, 0, 2).reshape(64, 1024))
    ssm_sl4 = np.stack([sl_c(c_re), sl_c(c_im), sl_b(b_re), sl_b(b_im)]).astype(np.float32)
    mask8 = np.zeros((128, 8), np.float32)
    for c in range(128):
        mask8[c, c // 16] = 1.0
    shared = {
        "w1_gate": f(inp["w1_gate"][L]), "w1_up": f(inp["w1_up"][L]), "w1_down": f(inp["w1_down"][L]),
        "w2_gate": f(inp["w2_gate"][L]), "w2_up": f(inp["w2_up"][L]), "w2_down": f(inp["w2_down"][L]),
        "w_in": f(inp["w_in"][L]), "w_out": f(inp["w_out"][L]), "w_q": f(inp["w_q"][L]), "w_k": f(inp["w_k"][L]),
        "w_v": f(inp["w_v"][L]), "w_o": f(inp["w_o"][L]), "w_glu": f(inp["w_glu"][L]), "w_pool": f(inp["w_pool"][L]),
        "gv": gv, "ssm_cl": ssm_cl, "ssm_sl3": ssm_sl3, "ssm_sl4": ssm_sl4, "mask8": mask8,
    }
    in_maps = []
    for c in range(8):
        b, half = c // 2, c % 2
        m = dict(shared)
        m["xT"] = np.ascontiguousarray(x[b, half * NT:(half + 1) * NT, :].T)
        m["memT"] = np.ascontiguousarray(mem[b].T)
        m["flag"] = np.full((128, 1), float(half), np.float32)
        ic = np.zeros((128, 4, 16), np.float32)
        for gi, w in enumerate((2, 4, 8, 16)):
            for t in range(16):
                ic[:, gi, t] = 1.0 / (min(t + 1, w) if half == 0 else w)
        m["invcnt"] = ic.reshape(128, 64)
        in_maps.append(m)
    return in_maps


_NC = {}


def kernel(**inputs):
    in_maps = prep_inputs(inputs)
    if "nc" not in _NC:
        _NC["nc"] = build_nc()
    res = run_bass_kernel_spmd(_NC["nc"], in_maps, core_ids=list(range(8)))
    out = np.empty((4, 4096, D), np.float32)
    for c in range(8):
        b, half = c // 2, c % 2
        out[b, half * NT:(half + 1) * NT, :] = res.results[c]["yT"].T
    return out
```
